# Optimizing a Trainium2 kernel written in Bass

```python
import math
import jax, jax.numpy as jnp
from jax import lax
import numpy as np

D_MODEL = 1024
BATCH = 8
SEQ = 4096
DEPTH = 4

N_Q_HEADS = 8
N_KV_HEADS = 2
HEAD_DIM = 64
ATTN_WIDTH = N_Q_HEADS * HEAD_DIM
KV_WIDTH = N_KV_HEADS * HEAD_DIM
Q_BLOCK = 128
ROPE_THETA = 10000.0
ROPE_AXIS_DIM = HEAD_DIM // 2
GRID_W = 64
SSM_WIDTH = D_MODEL // 2
SSM_GROUP = 16
SSM_GROUPS = SSM_WIDTH // SSM_GROUP
SSM_STATE = 64
N_DIRS = 2
DT_MIN = 1e-3
DT_MAX = 1e-1
LAMBDA_RE_MAX = -1e-4
MIX_WIDTH = ATTN_WIDTH + SSM_WIDTH
IN_WIDTH = ATTN_WIDTH + 2 * KV_WIDTH + SSM_WIDTH
FFN_HIDDEN = -(-8 * D_MODEL // (3 * 256)) * 256
N_MOD = 6
NORM_EPS = 1e-6

kernel_name = "hymba_style_gqa_s5_bidir_encoder"


def rms_norm(x, gain):
    xf = x.astype(jnp.float32)
    y = xf * lax.rsqrt(jnp.mean(xf * xf, axis=-1, keepdims=True) + NORM_EPS)
    return (y * gain.astype(jnp.float32)).astype(x.dtype)


def axial_rope_tables(seq_len):
    rows = seq_len // GRID_W
    row_idx = jnp.repeat(jnp.arange(rows, dtype=jnp.float32), GRID_W)
    col_idx = jnp.tile(jnp.arange(GRID_W, dtype=jnp.float32), rows)
    inv_freq = 1.0 / (ROPE_THETA ** (jnp.arange(0, ROPE_AXIS_DIM, 2, dtype=jnp.float32) / ROPE_AXIS_DIM))
    ang = jnp.concatenate([row_idx[:, None] * inv_freq, col_idx[:, None] * inv_freq], axis=-1)
    return jnp.cos(ang), jnp.sin(ang)


def apply_rope(x, cos, sin):
    xf = x.astype(jnp.float32).reshape(*x.shape[:-1], HEAD_DIM // 2, 2)
    x1, x2 = xf[..., 0], xf[..., 1]
    cs = cos[None, :, None, :]
    sn = sin[None, :, None, :]
    out = jnp.stack([x1 * cs - x2 * sn, x1 * sn + x2 * cs], axis=-1).reshape(x.shape)
    return out.astype(x.dtype)


def block_attention(q, k, v):
    bsz, seq_len = q.shape[0], q.shape[1]
    n_blocks = seq_len // Q_BLOCK
    rep = N_Q_HEADS // N_KV_HEADS
    scale = HEAD_DIM ** -0.5
    qb = q.reshape(bsz, n_blocks, Q_BLOCK, N_KV_HEADS, rep, HEAD_DIM).transpose(1, 0, 2, 3, 4, 5)

    def one_block(q_blk):
        s = jnp.einsum('bqgrd,bkgd->bgrqk', q_blk, k, preferred_element_type=jnp.float32) * scale
        p = jax.nn.softmax(s, axis=-1)
        return jnp.einsum('bgrqk,bkgd->bqgrd', p.astype(v.dtype), v)

    o = lax.map(one_block, qb)
    return o.transpose(1, 0, 2, 3, 4, 5).reshape(bsz, seq_len, ATTN_WIDTH)


def _ssm_combine(e1, e2):
    a1r, a1i, b1r, b1i = e1
    a2r, a2i, b2r, b2i = e2
    return (a2r * a1r - a2i * a1i,
            a2r * a1i + a2i * a1r,
            a2r * b1r - a2i * b1i + b2r,
            a2r * b1i + a2i * b1r + b2i)


def ssm_direction(u, lam_re, lam_im, log_dt, b_re, b_im, c_re, c_im, reverse):
    lr = jnp.minimum(lam_re.astype(jnp.float32), LAMBDA_RE_MAX)
    li = lam_im.astype(jnp.float32)
    dt = jnp.exp(log_dt.astype(jnp.float32))[:, None]
    mag = jnp.exp(lr * dt)
    a_re = mag * jnp.cos(li * dt)
    a_im = mag * jnp.sin(li * dt)
    den = lr * lr + li * li
    n_re = a_re - 1.0
    k_re = (n_re * lr + a_im * li) / den
    k_im = (a_im * lr - n_re * li) / den
    br = b_re.astype(jnp.float32)
    bi = b_im.astype(jnp.float32)
    bb_re = k_re[..., None] * br - k_im[..., None] * bi
    bb_im = k_re[..., None] * bi + k_im[..., None] * br
    x_re = jnp.einsum('gpc,blgc->blgp', bb_re, u)
    x_im = jnp.einsum('gpc,blgc->blgp', bb_im, u)
    a_re_b = jnp.broadcast_to(a_re, x_re.shape)
    a_im_b = jnp.broadcast_to(a_im, x_im.shape)
    _, _, h_re, h_im = lax.associative_scan(_ssm_combine, (a_re_b, a_im_b, x_re, x_im),
                                            reverse=reverse, axis=1)
    return (jnp.einsum('gcp,blgp->blgc', c_re.astype(jnp.float32), h_re)
            - jnp.einsum('gcp,blgp->blgc', c_im.astype(jnp.float32), h_im))


def ssm_mixer(u, lam_re, lam_im, log_dt, b_re, b_im, c_re, c_im, d_skip, w_glu, b_glu):
    bsz, seq_len = u.shape[0], u.shape[1]
    uf = u.astype(jnp.float32).reshape(bsz, seq_len, SSM_GROUPS, SSM_GROUP)
    y = ssm_direction(uf, lam_re[0], lam_im[0], log_dt[0], b_re[0], b_im[0], c_re[0], c_im[0], False)
    y = y + ssm_direction(uf, lam_re[1], lam_im[1], log_dt[1], b_re[1], b_im[1], c_re[1], c_im[1], True)
    y = y.reshape(bsz, seq_len, SSM_WIDTH) + d_skip.astype(jnp.float32) * uf.reshape(bsz, seq_len, SSM_WIDTH)
    y = jax.nn.gelu(y).astype(u.dtype)
    return y * jax.nn.sigmoid(y @ w_glu + b_glu)


def setup_inputs(seed: int = 0) -> dict:
    key = jax.random.key(seed)
    ks = jax.random.split(key, 24)
    f32 = jnp.float32
    nrm = lambda k, shape, s: jax.random.normal(k, shape, f32) * s
    G, P, C = SSM_GROUPS, SSM_STATE, SSM_GROUP
    lam_im_base = math.pi * jnp.arange(P, dtype=f32)
    return {
        "x": nrm(ks[0], (BATCH, SEQ, D_MODEL), 1.0),
        "c": nrm(ks[1], (BATCH, D_MODEL), 1.0),
        "w_ada": nrm(ks[2], (DEPTH, D_MODEL, N_MOD * D_MODEL), 0.5 * D_MODEL ** -0.5),
        "b_ada": nrm(ks[3], (DEPTH, N_MOD * D_MODEL), 0.01),
        "norm1": 1.0 + nrm(ks[4], (DEPTH, D_MODEL), 0.02),
        "w_in": nrm(ks[5], (DEPTH, D_MODEL, IN_WIDTH), D_MODEL ** -0.5),
        "q_norm": 1.0 + nrm(ks[6], (DEPTH, HEAD_DIM), 0.02),
        "k_norm": 1.0 + nrm(ks[7], (DEPTH, HEAD_DIM), 0.02),
        "ssm_lam_re": -0.5 + nrm(ks[8], (DEPTH, N_DIRS, G, P), 0.01),
        "ssm_lam_im": lam_im_base + nrm(ks[9], (DEPTH, N_DIRS, G, P), 0.01),
        "ssm_log_dt": jax.random.uniform(ks[10], (DEPTH, N_DIRS, G), f32, math.log(DT_MIN), math.log(DT_MAX)),
        "ssm_b_re": nrm(ks[11], (DEPTH, N_DIRS, G, P, C), (2.0 * C) ** -0.5),
        "ssm_b_im": nrm(ks[12], (DEPTH, N_DIRS, G, P, C), (2.0 * C) ** -0.5),
        "ssm_c_re": nrm(ks[13], (DEPTH, N_DIRS, G, C, P), (2.0 * P) ** -0.5),
        "ssm_c_im": nrm(ks[14], (DEPTH, N_DIRS, G, C, P), (2.0 * P) ** -0.5),
        "ssm_d": nrm(ks[15], (DEPTH, SSM_WIDTH), 1.0),
        "w_glu": nrm(ks[16], (DEPTH, SSM_WIDTH, SSM_WIDTH), SSM_WIDTH ** -0.5),
        "b_glu": nrm(ks[17], (DEPTH, SSM_WIDTH), 0.01),
        "attn_out_norm": 1.0 + nrm(ks[18], (DEPTH, ATTN_WIDTH), 0.02),
        "ssm_out_norm": 1.0 + nrm(ks[19], (DEPTH, SSM_WIDTH), 0.02),
        "w_out": nrm(ks[20], (DEPTH, MIX_WIDTH, D_MODEL), MIX_WIDTH ** -0.5),
        "norm2": 1.0 + nrm(ks[21], (DEPTH, D_MODEL), 0.02),
        "w_ffn_in": nrm(ks[22], (DEPTH, D_MODEL, 2 * FFN_HIDDEN), D_MODEL ** -0.5),
        "w_ffn_out": nrm(ks[23], (DEPTH, FFN_HIDDEN, D_MODEL), FFN_HIDDEN ** -0.5),
        "final_norm": 1.0 + nrm(jax.random.fold_in(key, 99), (D_MODEL,), 0.02),
    }


def reference(x, c, w_ada, b_ada, norm1, w_in, q_norm, k_norm, ssm_lam_re, ssm_lam_im, ssm_log_dt,
              ssm_b_re, ssm_b_im, ssm_c_re, ssm_c_im, ssm_d, w_glu, b_glu, attn_out_norm, ssm_out_norm,
              w_out, norm2, w_ffn_in, w_ffn_out, final_norm):
    bsz, seq_len = x.shape[0], x.shape[1]
    cos, sin = axial_rope_tables(seq_len)
    c_act = jax.nn.silu(c)
    split_pts = [ATTN_WIDTH, ATTN_WIDTH + KV_WIDTH, ATTN_WIDTH + 2 * KV_WIDTH]
    for i in range(DEPTH):
        mod = (c_act @ w_ada[i] + b_ada[i])[:, None, :]
        sh1, sc1, g1, sh2, sc2, g2 = jnp.split(mod, N_MOD, axis=-1)
        h = rms_norm(x, norm1[i]) * (1.0 + sc1) + sh1
        z = h @ w_in[i]
        q, k, v, u = jnp.split(z, split_pts, axis=-1)
        q = apply_rope(rms_norm(q.reshape(bsz, seq_len, N_Q_HEADS, HEAD_DIM), q_norm[i]), cos, sin)
        k = apply_rope(rms_norm(k.reshape(bsz, seq_len, N_KV_HEADS, HEAD_DIM), k_norm[i]), cos, sin)
        v = v.reshape(bsz, seq_len, N_KV_HEADS, HEAD_DIM)
        o_attn = block_attention(q, k, v)
        o_ssm = ssm_mixer(u, ssm_lam_re[i], ssm_lam_im[i], ssm_log_dt[i], ssm_b_re[i], ssm_b_im[i],
                          ssm_c_re[i], ssm_c_im[i], ssm_d[i], w_glu[i], b_glu[i])
        mix = jnp.concatenate([rms_norm(o_attn, attn_out_norm[i]), rms_norm(o_ssm, ssm_out_norm[i])], axis=-1)
        x = x + g1 * (mix @ w_out[i])
        h = rms_norm(x, norm2[i]) * (1.0 + sc2) + sh2
        gt, up = jnp.split(h @ w_ffn_in[i], 2, axis=-1)
        x = x + g2 * ((jax.nn.silu(gt) * up) @ w_ffn_out[i])
    return rms_norm(x, final_norm)
```

```python
import contextlib
import math
import numpy as np
import concourse.bass as bass
import concourse.mybir as mybir
from concourse.bass_utils import run_bass_kernel_spmd

F32 = mybir.dt.float32
BF16 = mybir.dt.bfloat16
I32 = mybir.dt.int32
AF = mybir.ActivationFunctionType
ALU = mybir.AluOpType
AX = mybir.AxisListType

ENGS = ("pe", "act", "dve", "pool", "sp")
SEQ = 4096
DM = 1024
NT = SEQ // 128
FH = 2816
TWO_PI = 2.0 * math.pi


class Buf:
    __slots__ = ("name", "w", "r", "excl")

    def __init__(self, name="", excl=False):
        self.name = name
        self.w = None
        self.r = {}
        self.excl = excl


class T:
    __slots__ = ("ap", "buf")

    def __init__(self, ap, buf):
        self.ap = ap
        self.buf = buf

    def __getitem__(self, idx):
        return T(self.ap[idx], self.buf)

    def re(self, pattern, **kw):
        return T(self.ap.rearrange(pattern, **kw), self.buf)

    def bc(self, axis, shape):
        return T(self.ap.unsqueeze(axis).to_broadcast(list(shape)), self.buf)

    def bitcast(self, dt):
        return T(self.ap.bitcast(dt), self.buf)


def dT(ap):
    return T(ap, Buf("dram"))


class Phase:
    def __init__(self, nc, name, n_dma_sems=24, same_engine_sync=True):
        self.nc = nc
        self.name = name
        self.ses = same_engine_sync
        self.stack = contextlib.ExitStack()
        self.ops = {e: [] for e in ENGS}
        self.cnt = {e: 0 for e in ENGS}
        self.seen = {e: {} for e in ENGS}
        self.sem = {e: self.stack.enter_context(nc.semaphore(f"{name}_{e}")) for e in ENGS}
        self.dsem = [self.stack.enter_context(nc.semaphore(f"{name}_d{i}")) for i in range(n_dma_sems)]
        self.dtot = [0] * n_dma_sems
        self.dnext = 0
        self.nbuf = 0
        self.rr = 0

    def sb(self, shape, dtype, name=None):
        self.nbuf += 1
        name = name or f"t{self.nbuf}"
        t = self.stack.enter_context(self.nc.sbuf_tensor(f"{self.name}_{name}", list(shape), dtype))
        return T(t[:], Buf(name))

    def ps(self, shape, dtype=F32, name=None):
        self.nbuf += 1
        name = name or f"p{self.nbuf}"
        t = self.stack.enter_context(self.nc.psum_tensor(f"{self.name}_{name}", list(shape), dtype))
        tt_ = T(t[:], Buf(name, excl=True))
        self.memset("dve", tt_, 0.0)
        return tt_

    def wrap(self, ap):
        return T(ap, Buf("g"))

    def _wait(self, eng, tok):
        key, sem, val = tok
        if key == eng and (eng == "pe" or not self.ses):
            return
        if self.seen[eng].get(key, 0) >= val:
            return
        self.seen[eng][key] = val
        self.ops[eng].append(lambda e, s=sem, v=val: e.wait_ge(s, v))

    def _deps(self, eng, reads, writes):
        for b in reads:
            if b.w is not None:
                self._wait(eng, b.w)
        for b in writes:
            if b.w is not None:
                self._wait(eng, b.w)
            for tok in b.r.values():
                self._wait(eng, tok)

    def _update(self, tok, reads, writes):
        for b in reads:
            b.r[tok[0]] = tok
        for b in writes:
            b.w = tok
            b.r = {}

    def issue(self, eng, fn, reads, writes):
        reads = [x.buf if isinstance(x, T) else x for x in reads]
        writes = [x.buf if isinstance(x, T) else x for x in writes]
        writes = writes + [b for b in reads if b.excl and b not in writes]
        self._deps(eng, reads, writes)
        self.cnt[eng] += 1
        tok = (eng, self.sem[eng], self.cnt[eng])
        s = self.sem[eng]
        self.ops[eng].append(lambda e, fn=fn, s=s: fn(e).then_inc(s, 1))
        self._update(tok, reads, writes)

    def dma(self, eng, out, in_, **kw):
        reads = [in_.buf]
        writes = [out.buf]
        self._deps(eng, reads, writes)
        j = self.dnext
        self.dnext = (self.dnext + 1) % len(self.dsem)
        key = f"d{j}"
        if self.dtot[j] > 0:
            self._wait(eng, (key, self.dsem[j], self.dtot[j]))
        self.dtot[j] += 16
        tok = (key, self.dsem[j], self.dtot[j])
        s = self.dsem[j]
        oa, ia = out.ap, in_.ap
        self.ops[eng].append(lambda e, oa=oa, ia=ia, s=s, kw=kw: e.dma_start(out=oa, in_=ia, **kw).then_inc(s, 16))
        self._update(tok, reads, writes)

    def finish(self):
        for j, tot in enumerate(self.dtot):
            if tot > 0:
                self._wait("sp", (f"d{j}", self.dsem[j], tot))
        nc = self.nc
        ops = self.ops
        sems = list(self.sem.values()) + list(self.dsem)
        with nc.Block() as b0:
            @b0.sync
            def _(e):
                for s_ in sems:
                    e.sem_clear(s_)
        with nc.Block() as block:
            @block.tensor
            def _(e):
                for f in ops["pe"]:
                    f(e)

            @block.scalar
            def _(e):
                for f in ops["act"]:
                    f(e)

            @block.vector
            def _(e):
                for f in ops["dve"]:
                    f(e)

            @block.gpsimd
            def _(e):
                for f in ops["pool"]:
                    f(e)

            @block.sync
            def _(e):
                for f in ops["sp"]:
                    f(e)
        self.stack.close()

    def matmul(self, out, lhsT, rhs, start=True, stop=True, **kw):
        self.issue("pe", lambda e: e.matmul(out.ap, lhsT.ap, rhs.ap, start=start, stop=stop, **kw),
                   [lhsT, rhs], [out])

    def transpose(self, out, in_, ident):
        self.issue("pe", lambda e: e.transpose(out.ap, in_.ap, ident.ap), [in_, ident], [out])

    def act(self, out, in_, func, scale=1.0, bias=None, accum_out=None):
        reads = [in_]
        writes = [out]
        kw = {}
        if isinstance(bias, T):
            reads.append(bias)
            kw["bias"] = bias.ap
        elif bias is not None:
            kw["bias"] = bias
        if isinstance(scale, T):
            reads.append(scale)
            kw["scale"] = scale.ap
        else:
            kw["scale"] = scale
        if accum_out is not None:
            writes.append(accum_out)
            kw["accum_out"] = accum_out.ap
        self.issue("act", lambda e: e.activation(out=out.ap, in_=in_.ap, func=func, **kw), reads, writes)

    def tt(self, eng, out, in0, in1, op):
        self.issue(eng, lambda e: e.tensor_tensor(out=out.ap, in0=in0.ap, in1=in1.ap, op=op), [in0, in1], [out])

    def ts(self, eng, out, in0, s1, op0, s2=None, op1=None):
        reads = [in0]
        a1 = s1.ap if isinstance(s1, T) else s1
        a2 = s2.ap if isinstance(s2, T) else s2
        if isinstance(s1, T):
            reads.append(s1)
        if isinstance(s2, T):
            reads.append(s2)
        kw = {}
        if op1 is not None:
            kw["op1"] = op1
        self.issue(eng, lambda e: e.tensor_scalar(out=out.ap, in0=in0.ap, scalar1=a1, scalar2=a2, op0=op0, **kw),
                   reads, [out])

    def stt(self, out, in0, scalar, in1, op0, op1):
        reads = [in0, in1]
        a = scalar.ap if isinstance(scalar, T) else scalar
        if isinstance(scalar, T):
            reads.append(scalar)
        self.issue("dve", lambda e: e.scalar_tensor_tensor(out=out.ap, in0=in0.ap, scalar=a, in1=in1.ap, op0=op0, op1=op1),
                   reads, [out])

    def copy(self, eng, out, in_):
        if eng == "act":
            self.issue(eng, lambda e: e.copy(out=out.ap, in_=in_.ap), [in_], [out])
        else:
            self.issue(eng, lambda e: e.tensor_copy(out=out.ap, in_=in_.ap), [in_], [out])

    def memset(self, eng, out, val):
        self.issue(eng, lambda e: e.memset(out.ap, val), [], [out])

    def reduce(self, out, in_, op=None, axis=None):
        op = op or ALU.add
        axis = axis or AX.X
        self.issue("dve", lambda e: e.tensor_reduce(out=out.ap, in_=in_.ap, axis=axis, op=op), [in_], [out])

    def recip(self, out, in_):
        self.issue("dve", lambda e: e.reciprocal(out=out.ap, in_=in_.ap), [in_], [out])

    def rstd(self, out, ss, n, mhalf, mode="sqrt"):
        if mode == "sqrt":
            self.act(out, ss, AF.Sqrt, scale=1.0 / n, bias=self.epsT(ss))
            self.recip(out, out)
            return
        shp = list(ss.ap.shape)
        a = self.sb(shp, F32); y = self.sb(shp, F32); t = self.sb(shp, F32)
        self.ts("dve", a, ss, 1.0 / n, ALU.mult, 1e-6, ALU.add)
        ai = a.bitcast(I32); yi = y.bitcast(I32)
        self.issue("dve", lambda e: e.tensor_scalar(out=yi.ap, in0=ai.ap, scalar1=1, scalar2=None, op0=ALU.arith_shift_right), [a], [y])
        self.issue("dve", lambda e: e.tensor_scalar(out=yi.ap, in0=yi.ap, scalar1=-1, scalar2=None, op0=ALU.bitwise_xor), [y], [y])
        self.issue("dve", lambda e: e.tensor_scalar_add_int(yi.ap, yi.ap, 0x5f3759df + 1), [y], [y])
        for it in range(2):
            self.tt("dve", t, y, y, ALU.mult)
            self.tt("dve", t, t, a, ALU.mult)
            self.ts("dve", t, t, -0.5, ALU.mult, 1.5, ALU.add)
            self.tt("dve", out if it == 1 else y, y, t, ALU.mult)

    def epsT(self, like):
        if not hasattr(self, "_eps"):
            self._eps = self.sb([128, 1], F32, name="epsc")
            self.memset("pool", self._eps, 1e-6)
        return self._eps[0:like.ap.shape[0], :]

    def eng3(self):
        self.rr += 1
        return ("dve", "pool", "act")[self.rr % 3]

    def scale_rows(self, out, in_, col):
        e = self.eng3()
        if e == "act":
            self.act(out, in_, AF.Copy, scale=col)
        else:
            self.ts(e, out, in_, col, ALU.mult)


def build(NL=4, dbg=False):
    nc = bass.Bass("TRN2", target_bir_lowering=False)
    D = {}

    def din(name, shape, dt=F32):
        D[name] = nc.dram_tensor(name, list(shape), dt, kind="ExternalInput").ap()

    def dscr(name, shape, dt, out=False):
        D[name] = nc.dram_tensor(name, list(shape), dt, kind=("ExternalOutput" if out else "Internal")).ap()

    L = 4
    din("x", [SEQ, DM]); din("cT", [128, 8])
    din("w_ada", [L, DM, 6144]); din("b_ada_row", [L, 1, 6144])
    din("norm1_col", [L, 128, 8]); din("norm2_col", [L, 128, 8]); din("mixn_col", [L, 128, 8])
    din("w_in", [L, DM, 1280]); din("w_out", [L, DM, DM]); din("w_ffn_in", [L, DM, 2 * FH])
    din("w_ffn_out", [L, FH, DM]); din("w_glu", [L, 512, 512]); din("b_glu_row", [L, 1, 512])
    din("gain_qk", [L, 1, 640]); din("fnorm_row", [1, DM])
    din("lam_re", [L, 128, 32]); din("lam_im", [L, 128, 32]); din("log_dt", [L, 128, 32])
    din("b_re", [L, 128, 512]); din("b_im", [L, 128, 512]); din("c_re", [L, 128, 512]); din("c_im", [L, 128, 512])
    din("dvec", [L, 128, 32])
    din("ident", [128, 128]); din("maskf", [128, 512]); din("maskb", [128, 512])
    din("ropec", [128, NT * 32]); din("ropes", [128, NT * 32]); din("mexp", [128, 34])

    dscr("out", [SEQ, DM], F32, out=True)
    dscr("win_s", [L, DM, 1280], BF16); dscr("bz_s", [L, 1, 1280], BF16)
    dscr("wout_s", [L, DM, DM], BF16); dscr("wfi_s", [L, DM, 2 * FH], BF16); dscr("bfi_s", [L, 128, 44], F32)
    dscr("wfo_s", [L, FH, DM], BF16); dscr("wglu_s", [L, 512, 512], BF16)
    dscr("ssm_s", [L, 32, 128, 1536], BF16, out=dbg); dscr("dec_s", [L, 128, 128], F32, out=dbg)
    dscr("u_scr", [SEQ, 512], BF16, out=dbg); dscr("yg_scr", [SEQ, 512], BF16, out=dbg)
    dscr("mixa_scr", [128, 4, SEQ], BF16, out=dbg)
    dscr("xmid", [SEQ, DM], F32, out=dbg); dscr("xcur", [SEQ, DM], F32, out=dbg)
    if dbg:
        dscr("dbg_z", [SEQ, 1280], F32, out=True)

    gst = contextlib.ExitStack()

    def gsb(name, shape, dt):
        return gst.enter_context(nc.sbuf_tensor(name, list(shape), dt))[:]

    G = dict(identf=gsb("g_identf", [128, 128], F32), identb=gsb("g_identb", [128, 128], BF16),
             onesf=gsb("g_onesf", [1, 128], F32), onesb=gsb("g_onesb", [1, 128], BF16),
             cact=gsb("g_cact", [128, 8], F32), mhalf=gsb("g_mhalf", [128, 16], F32),
             e0=gsb("g_e0", [128, 128], BF16))

    ph = Phase(nc, "G0")
    identf, identb, onesf, onesb, cact, mhalf = (ph.wrap(G[k]) for k in ("identf", "identb", "onesf", "onesb", "cact", "mhalf"))
    ph.dma("sp", identf, dT(D["ident"]))
    ph.copy("dve", identb, identf)
    ph.memset("pool", onesf, 1.0)
    ph.memset("pool", onesb, 1.0)
    ph.memset("pool", mhalf, -0.5)
    e0 = ph.wrap(G["e0"])
    ph.memset("pool", e0, 0.0)
    ph.memset("pool", e0[0:1, :], 1.0)
    ct = ph.sb([128, 8], F32)
    th = ph.sb([128, 8], F32)
    ph.dma("sp", ct, dT(D["cT"]))
    ph.act(th, ct, AF.Tanh, scale=0.5)
    ph.ts("dve", th, th, 0.5, ALU.mult, 0.5, ALU.add)
    ph.tt("dve", cact, ct, th, ALU.mult)
    ph.finish()

    import os
    upto = os.environ.get("UPTO", "F")
    order = "STACDF"
    for l in range(NL):
        fns = [lambda: setup_layer(nc, l, G, D), lambda: setup_ssm(nc, l, G, D), lambda: phase_ab(nc, l, G, D, dbg),
               lambda: phase_c(nc, l, G, D), lambda: phase_d1(nc, l, G, D), lambda: phase_d2(nc, l, G, D, last=(l == NL - 1))]
        for ch, fn in zip(order, fns):
            if order.index(ch) <= order.index(upto):
                fn()
    gst.close()
    return nc


def wrapG(ph, G):
    return {k: ph.wrap(v) for k, v in G.items()}


def setup_layer(nc, l, G, D):
    ph = Phase(nc, f"S{l}")
    g = wrapG(ph, G)
    cact, onesf, identf, identb = g["cact"], g["onesf"], g["identf"], g["identb"]

    wa = D["w_ada"][l].rearrange("(kt p) n -> p kt n", p=128)
    modrow = ph.sb([1, 6144], F32)
    ph.dma("sp", modrow, dT(D["b_ada_row"][l]))
    wt = [ph.sb([128, 8, 512], F32, name=f"wt{i}") for i in range(2)]
    wb = [ph.sb([128, 8, 512], BF16, name=f"wb{i}") for i in range(2)]
    pm = [ph.ps([128, 512], F32, name=f"pm{i}") for i in range(4)]
    nld = [0]
    nst = [0]

    pend = []

    def flush():
        while pend:
            d_, s_ = pend.pop(0)
            ph.dma("sp", d_, s_)

    def ld(src_ap, cw, nk=8):
        t = wt[nld[0] % 2]
        nld[0] += 1
        ph.dma("sp", t[:, :nk, :cw], dT(src_ap))
        flush()
        return t

    def st(dst_ap, cw):
        t = wb[nst[0] % 2]
        nst[0] += 1
        return t

    for j in range(12):
        t = ld(wa[:, :, j * 512:(j + 1) * 512], 512)
        p = pm[j % 2]
        for kt in range(8):
            ph.matmul(p[0:1, :], cact[:, kt:kt + 1], t[:, kt, :], start=(kt == 0), stop=(kt == 7))
        ph.tt("dve", modrow[:, j * 512:(j + 1) * 512], p[0:1, :], modrow[:, j * 512:(j + 1) * 512], ALU.add)
    one11 = onesf[0:1, 0:1]
    pcol = pm[2]
    cols = list(range(0, 16)) + list(range(24, 40))
    for j in cols:
        ph.matmul(pcol[:, j:j + 1], modrow[0:1, j * 128:(j + 1) * 128], one11, start=True, stop=True)
    modcol = ph.sb([128, 48], F32)
    ph.copy("dve", modcol[:, 0:16], pcol[:, 0:16])
    ph.copy("dve", modcol[:, 24:40], pcol[:, 24:40])
    n1c = ph.sb([128, 8], F32); n2c = ph.sb([128, 8], F32); mixc = ph.sb([128, 8], F32)
    ph.dma("sp", n1c, dT(D["norm1_col"][l])); ph.dma("sp", n2c, dT(D["norm2_col"][l])); ph.dma("sp", mixc, dT(D["mixn_col"][l]))
    s1c = ph.sb([128, 8], F32); s2c = ph.sb([128, 8], F32)
    ph.stt(s1c, modcol[:, 8:16], 1.0, n1c, ALU.add, ALU.mult)
    ph.stt(s2c, modcol[:, 32:40], 1.0, n2c, ALU.add, ALU.mult)
    sh1c = modcol[:, 0:8]
    sh2c = modcol[:, 24:32]
    g1b = ph.sb([128, 1024], F32); g2b = ph.sb([128, 1024], F32)
    for (gb, off, sc) in ((g1b, 2048, 1.0), (g2b, 5120, 0.5)):
        for c2 in range(2):
            p = pm[c2]
            ph.matmul(p, onesf, modrow[0:1, off + c2 * 512: off + (c2 + 1) * 512], start=True, stop=True)
            ph.act(gb[:, c2 * 512:(c2 + 1) * 512], p, AF.Copy, scale=sc)

    wi = D["w_in"][l].rearrange("(kt p) n -> p kt n", p=128)
    wis = D["win_s"][l].rearrange("(kt p) n -> p kt n", p=128)
    bzrow = ph.sb([1, 1280], BF16)
    for ci, (c0, cw) in enumerate(((0, 512), (512, 512), (1024, 256))):
        t = ld(wi[:, :, c0:c0 + cw], cw)
        p = pm[ci % 2]
        for kt in range(8):
            ph.matmul(p[0:1, :cw], sh1c[:, kt:kt + 1], t[:, kt, :cw], start=(kt == 0), stop=(kt == 7))
        ph.copy("act", bzrow[:, c0:c0 + cw], p[0:1, :cw])
        o = st(None, cw)
        for kt in range(8):
            ph.scale_rows(o[:, kt, :cw], t[:, kt, :cw], s1c[:, kt:kt + 1])
        pend.append((dT(wis[:, :, c0:c0 + cw]), o[:, :, :cw]))
    ph.dma("sp", dT(D["bz_s"][l]), bzrow)

    wo = D["w_out"][l].rearrange("(kt p) n -> p kt n", p=128)
    wos = D["wout_s"][l].rearrange("(kt p) n -> p kt n", p=128)
    for c2 in range(2):
        t = ld(wo[:, :, c2 * 512:(c2 + 1) * 512], 512)
        o = st(None, 512)
        for kt in range(8):
            ph.stt(o[:, kt, :], t[:, kt, :], mixc[:, kt:kt + 1], g1b[:, c2 * 512:(c2 + 1) * 512], ALU.mult, ALU.mult)
        pend.append((dT(wos[:, :, c2 * 512:(c2 + 1) * 512]), o))

    wf = D["w_ffn_in"][l].rearrange("(kt p) n -> p kt n", p=128)
    wfs = D["wfi_s"][l].rearrange("(kt p) n -> p kt n", p=128)
    pbc = pm[3]
    for j in range(11):
        t = ld(wf[:, :, j * 512:(j + 1) * 512], 512)
        for f4 in range(4):
            for kt in range(8):
                ph.matmul(pbc[:, j * 4 + f4: j * 4 + f4 + 1], t[:, kt, f4 * 128:(f4 + 1) * 128], sh2c[:, kt:kt + 1],
                          start=(kt == 0), stop=(kt == 7))
        o = st(None, 512)
        for kt in range(8):
            ph.scale_rows(o[:, kt, :], t[:, kt, :], s2c[:, kt:kt + 1])
        pend.append((dT(wfs[:, :, j * 512:(j + 1) * 512]), o))
    bfi = ph.sb([128, 44], F32)
    ph.copy("dve", bfi, pbc[:, 0:44])
    ph.dma("sp", dT(D["bfi_s"][l]), bfi)

    wfo = D["w_ffn_out"][l].rearrange("(ft p) n -> p ft n", p=128)
    wfos = D["wfo_s"][l].rearrange("(ft p) n -> p ft n", p=128)
    for c2 in range(2):
        for f0 in range(0, 22, 8):
            nf = min(8, 22 - f0)
            t = ld(wfo[:, f0:f0 + nf, c2 * 512:(c2 + 1) * 512], 512, nf)
            o = st(None, 512)
            e = "dve" if (f0 // 8) % 2 == 0 else "pool"
            ph.tt(e, o[:, :nf, :], t[:, :nf, :], g2b[:, c2 * 512:(c2 + 1) * 512].bc(1, [128, nf, 512]), ALU.mult)
            pend.append((dT(wfos[:, f0:f0 + nf, c2 * 512:(c2 + 1) * 512]), o[:, :nf, :]))
    wg = D["w_glu"][l].rearrange("(kt p) n -> p kt n", p=128)
    wgs = D["wglu_s"][l].rearrange("(kt p) n -> p kt n", p=128)
    t = ld(wg, 512, 4)
    o = st(None, 512)
    ph.copy("pool", o[:, 0:4, :], t[:, 0:4, :])
    pend.append((dT(wgs), o[:, 0:4, :]))

    flush()
    ph.finish()


def setup_ssm(nc, l, G, D):
    ph = Phase(nc, f"T{l}")
    g = wrapG(ph, G)
    identf, identb = g["identf"], g["identb"]
    pm = [ph.ps([128, 512], F32, name=f"pm{i}") for i in range(4)]
    lre = ph.sb([128, 32], F32); lim = ph.sb([128, 32], F32); ldt = ph.sb([128, 32], F32)
    ph.dma("sp", lre, dT(D["lam_re"][l])); ph.dma("sp", lim, dT(D["lam_im"][l])); ph.dma("sp", ldt, dT(D["log_dt"][l]))
    mexp = ph.sb([128, 34], F32)
    ph.dma("sp", mexp, dT(D["mexp"]))
    bre = ph.sb([128, 32, 16], F32); bim = ph.sb([128, 32, 16], F32)
    cre = ph.sb([128, 32, 16], F32); cim = ph.sb([128, 32, 16], F32)
    ph.dma("sp", bre, dT(D["b_re"][l].rearrange("p (g c) -> p g c", c=16)))
    ph.dma("sp", bim, dT(D["b_im"][l].rearrange("p (g c) -> p g c", c=16)))
    ph.dma("sp", cre, dT(D["c_re"][l].rearrange("p (g c) -> p g c", c=16)))
    ph.dma("sp", cim, dT(D["c_im"][l].rearrange("p (g c) -> p g c", c=16)))
    dvec = ph.sb([128, 32], F32)
    ph.dma("sp", dvec, dT(D["dvec"][l]))
    maskf = ph.sb([128, 2, 256], F32); maskb = ph.sb([128, 2, 256], F32)
    ph.dma("sp", maskf, dT(D["maskf"].rearrange("p (s c) -> p s c", s=2)))
    ph.dma("sp", maskb, dT(D["maskb"].rearrange("p (s c) -> p s c", s=2)))

    lr = ph.sb([128, 32], F32); dtt = ph.sb([128, 32], F32)
    ph.ts("dve", lr, lre, -1e-4, ALU.min)
    ph.act(dtt, ldt, AF.Exp)
    lrdt = ph.sb([128, 32], F32); lidt = ph.sb([128, 32], F32)
    ph.tt("dve", lrdt, lr, dtt, ALU.mult)
    ph.tt("dve", lidt, lim, dtt, ALU.mult)
    NE = 34
    E = ph.sb([128, 32, NE], F32); ANG = ph.sb([128, 32, NE], F32)
    mexpb = mexp.bc(1, [128, 32, NE])
    ph.tt("dve", E, lrdt.bc(2, [128, 32, NE]), mexpb, ALU.mult)
    ph.tt("dve", ANG, lidt.bc(2, [128, 32, NE]), mexpb, ALU.mult)
    mag = E
    ph.act(mag, E, AF.Exp)
    ki = ph.sb([128, 32, NE], I32); kf = ph.sb([128, 32, NE], F32); r = ph.sb([128, 32, NE], F32)
    msk = ph.sb([128, 32, NE], F32)
    ph.ts("dve", kf, ANG, 1.0 / TWO_PI, ALU.mult)
    ph.copy("dve", ki, kf)
    ph.copy("dve", kf, ki)
    ph.stt(r, kf, -TWO_PI, ANG, ALU.mult, ALU.add)

    def fold(x):
        ph.ts("dve", msk, x, math.pi, ALU.is_gt, -TWO_PI, ALU.mult)
        ph.tt("dve", x, x, msk, ALU.add)
        ph.ts("dve", msk, x, -math.pi, ALU.is_lt, TWO_PI, ALU.mult)
        ph.tt("dve", x, x, msk, ALU.add)

    fold(r)
    sn = ph.sb([128, 32, NE], F32); cs = ph.sb([128, 32, NE], F32)
    ph.act(sn, r, AF.Sin)
    ph.ts("dve", r, r, math.pi / 2, ALU.add)
    fold(r)
    ph.act(cs, r, AF.Sin)
    Are = cs; Aim = sn
    ph.tt("dve", Are, mag, cs, ALU.mult)
    ph.tt("dve", Aim, mag, sn, ALU.mult)
    a1r = Are[:, :, 33]; a1i = Aim[:, :, 33]
    nre = ph.sb([128, 32], F32); den = ph.sb([128, 32], F32); t1 = ph.sb([128, 32], F32); t2 = ph.sb([128, 32], F32)
    kre = ph.sb([128, 32], F32); kim = ph.sb([128, 32], F32)
    ph.ts("dve", nre, a1r, -1.0, ALU.add)
    ph.tt("dve", den, lr, lr, ALU.mult)
    ph.tt("dve", t1, lim, lim, ALU.mult)
    ph.tt("dve", den, den, t1, ALU.add)
    ph.recip(den, den)
    ph.tt("dve", t1, nre, lr, ALU.mult)
    ph.tt("dve", t2, a1i, lim, ALU.mult)
    ph.tt("dve", t1, t1, t2, ALU.add)
    ph.tt("dve", kre, t1, den, ALU.mult)
    ph.tt("dve", t1, a1i, lr, ALU.mult)
    ph.tt("dve", t2, nre, lim, ALU.mult)
    ph.tt("dve", t1, t1, t2, ALU.subtract)
    ph.tt("dve", kim, t1, den, ALU.mult)
    bbr = ph.sb([128, 32, 16], F32); bbi = ph.sb([128, 32, 16], F32)
    u1 = ph.sb([128, 32, 16], F32); u2 = ph.sb([128, 32, 16], F32)
    kreb = kre.bc(2, [128, 32, 16]); kimb = kim.bc(2, [128, 32, 16])
    ph.tt("dve", u1, bre, kreb, ALU.mult)
    ph.tt("dve", u2, bim, kimb, ALU.mult)
    ph.tt("dve", bbr, u1, u2, ALU.subtract)
    ph.tt("dve", u1, bim, kreb, ALU.mult)
    ph.tt("dve", u2, bre, kimb, ALU.mult)
    ph.tt("dve", bbi, u1, u2, ALU.add)
    dec = ph.sb([128, 2, 32, 2], F32)
    a16r = Are[:, :, 32]; a16i = Aim[:, :, 32]
    ph.copy("dve", dec[:, 0, :, 0], a16r)
    ph.copy("dve", dec[:, 0, :, 1], a16r)
    ph.ts("dve", dec[:, 1, :, 0], a16i, -1.0, ALU.mult)
    ph.copy("dve", dec[:, 1, :, 1], a16i)
    ph.dma("sp", dT(D["dec_s"][l].rearrange("p (a g r) -> p a g r", a=2, r=2)), dec)

    GB = 4
    Fre = ph.sb([128, GB, 16, 16], BF16); Fim = ph.sb([128, GB, 16, 16], BF16)
    Gre = ph.sb([128, GB, 16, 16], BF16); nGim = ph.sb([128, GB, 16, 16], BF16)
    v1 = ph.sb([128, GB, 16, 16], F32); v2 = ph.sb([128, GB, 16, 16], F32)
    w1 = ph.sb([128, GB, 16, 16], F32); w2 = ph.sb([128, GB, 16, 16], F32)
    mt32 = [ph.sb([128, 256], F32, name=f"mt32_{i}") for i in range(2)]
    mt32b = [ph.sb([128, 256], F32, name=f"mt32b_{i}") for i in range(2)]
    pack = [ph.sb([128, 1024], BF16, name=f"pack{i}") for i in range(2)]
    shp = [128, GB, 16, 16]
    for gb in range(32 // GB):
        gs = slice(gb * GB, (gb + 1) * GB)
        aFr = T(Are.ap[:, gs, 0:16].unsqueeze(3).to_broadcast(shp), Are.buf)
        aFi = T(Aim.ap[:, gs, 0:16].unsqueeze(3).to_broadcast(shp), Aim.buf)
        bR = T(bbr.ap[:, gs, :].unsqueeze(2).to_broadcast(shp), bbr.buf)
        bI = T(bbi.ap[:, gs, :].unsqueeze(2).to_broadcast(shp), bbi.buf)
        ph.tt("dve", v1, aFr, bR, ALU.mult)
        ph.tt("dve", v2, aFi, bI, ALU.mult)
        ph.tt("dve", Fre, v1, v2, ALU.subtract)
        ph.tt("dve", v1, aFr, bI, ALU.mult)
        ph.tt("dve", v2, aFi, bR, ALU.mult)
        ph.tt("dve", Fim, v1, v2, ALU.add)
        aGr = T(Are.ap[:, gs, 16:32].unsqueeze(3).to_broadcast(shp), Are.buf)
        aGi = T(Aim.ap[:, gs, 16:32].unsqueeze(3).to_broadcast(shp), Aim.buf)
        cR = T(cre.ap[:, gs, :].unsqueeze(2).to_broadcast(shp), cre.buf)
        cI = T(cim.ap[:, gs, :].unsqueeze(2).to_broadcast(shp), cim.buf)
        ph.tt("dve", w1, aGr, cR, ALU.mult)
        ph.tt("dve", w2, aGi, cI, ALU.mult)
        ph.tt("dve", Gre, w1, w2, ALU.subtract)
        ph.tt("dve", w1, aGr, cI, ALU.mult)
        ph.tt("dve", w2, aGi, cR, ALU.mult)
        ph.tt("dve", w1, w1, w2, ALU.add)
        ph.ts("dve", nGim, w1, -1.0, ALU.mult)
        for gi in range(GB):
            gg = gb * GB + gi
            pk = pack[gg % 2]
            dst = D["ssm_s"][l][gg]
            for sh in range(2):
                pp = []
                for d in range(2):
                    p = pm[d]
                    ps_ = slice(64 * d, 64 * d + 64)
                    lre_ = Fre[ps_, gi, 8 * sh:8 * sh + 8, :].re("p s c -> p (s c)")
                    lim_ = Fim[ps_, gi, 8 * sh:8 * sh + 8, :].re("p s c -> p (s c)")
                    rre_ = Gre[ps_, gi, :, :].re("p t c -> p (t c)")
                    rim_ = nGim[ps_, gi, :, :].re("p t c -> p (t c)")
                    ph.matmul(p[:, 0:256], lre_, rre_, start=True, stop=False)
                    ph.matmul(p[:, 0:256], lim_, rim_, start=False, stop=True)
                    pp.append(p)
                a = mt32[sh]
                b = mt32b[sh]
                ph.tt("dve", a, pp[0][:, 0:256], maskf[:, sh, :], ALU.mult)
                ph.stt(a[:, sh * 128:(sh + 1) * 128], identf, dvec[:, gg:gg + 1], a[:, sh * 128:(sh + 1) * 128], ALU.mult, ALU.add)
                ph.tt("dve", b, pp[1][:, 0:256], maskb[:, sh, :], ALU.mult)
                ph.tt("pool", pk[:, sh * 256:(sh + 1) * 256], a, b, ALU.add)
            pt = pm[2].bitcast(BF16)
            for sh in range(2):
                for ri, Fx in enumerate((Fre, Fim)):
                    ph.transpose(pt[:, (sh * 2 + ri) * 128:(sh * 2 + ri + 1) * 128],
                                 Fx[:, gi, 8 * sh:8 * sh + 8, :].re("p s c -> p (s c)"), identb)
            ph.copy("act", pk[:, 512:1024], pt[:, 0:512])
            ph.dma("sp", dT(dst[:, 0:1024]), pk)
            ph.dma("sp", dT(dst[:, 1024:1280]), Gre[:, gi, :, :].re("p t c -> p (t c)"))
            ph.dma("sp", dT(dst[:, 1280:1536]), nGim[:, gi, :, :].re("p t c -> p (t c)"))
    ph.finish()


def phase_ab(nc, l, G, D, dbg):
    import os
    ph = Phase(nc, f"A{l}")
    g = wrapG(ph, G)
    identb, identf, mhalf, e0 = g["identb"], g["identf"], g["mhalf"], g["e0"]
    xsrc = D["x"] if l == 0 else D["xcur"]
    win = ph.sb([128, 8, 1280], BF16)
    ph.dma("sp", win, dT(D["win_s"][l].rearrange("(kt p) n -> p kt n", p=128)))
    bz = ph.sb([128, 1280], BF16)
    ph.memset("pool", bz, 0.0)
    ph.dma("sp", bz[0:1, :], dT(D["bz_s"][l]))
    gqk = ph.sb([128, 640], F32)
    ph.dma("sp", gqk, dT(D["gain_qk"][l].broadcast_to([128, 640])))
    cosT = ph.sb([128, NT, 32], F32); sinT = ph.sb([128, NT, 32], F32)
    ph.dma("sp", cosT, dT(D["ropec"].rearrange("p (i j) -> p i j", j=32)))
    ph.dma("sp", sinT, dT(D["ropes"].rearrange("p (i j) -> p i j", j=32)))
    qT = ph.sb([128, 4, SEQ], BF16)
    kTp = [ph.sb([128, SEQ], BF16, name=f"kTp{i}") for i in range(2)]
    ph.memset("pool", kTp[0][64:128, :], 0.0)
    ph.memset("pool", kTp[1][0:64, :], 0.0)
    Vx = ph.sb([128, NT, 2, 65], BF16)
    ph.memset("pool", Vx[:, :, :, 64:65], 1.0)
    S = [ph.ps([128, 1024], F32, name=f"S{i}") for i in range(3)]
    PA = ph.ps([128, 512], F32, name="PA")
    PB = ph.ps([128, 512], F32, name="PB")
    Z = [S[0][:, 0:512], S[0][:, 512:1024], S[1][:, 0:512]]

    def two(shape, dt, nm):
        return [ph.sb(shape, dt, name=f"{nm}{i}") for i in range(2)]
    xb = [ph.sb([128, DM], F32, name=f"xb{i}") for i in range(3)]
    junk_ = two([128, DM], BF16, "junk"); xn_ = two([128, DM], BF16, "xn")
    xnT = two([128, 8, 128], BF16, "xnT")
    ss_ = two([128, 1], F32, "ss"); rs_ = two([128, 1], F32, "rs")
    zqk_ = [ph.sb([128, 640], F32, name=f"zqk{i}") for i in range(3)]
    sq_ = two([128, 640], F32, "sq"); qn_ = two([128, 640], F32, "qn")
    ssh_ = two([128, 10], F32, "ssh"); rh_ = two([128, 10], F32, "rh")
    r1_ = two([128, 10, 32], F32, "r1"); r2_ = two([128, 10, 32], F32, "r2")
    r3_ = two([128, 10, 32], F32, "r3"); r4_ = two([128, 10, 32], F32, "r4")
    qr_ = two([128, 640], BF16, "qr")
    ub = two([128, 512], BF16, "ub")
    if dbg:
        zfb = ph.sb([128, 1280], F32, name="zfb")
    NTA = int(os.environ.get("NTA", NT)); ATT = int(os.environ.get("ATT", NT))

    def load_x(i):
        ph.dma("sp", xb[i % 3], dT(xsrc[i * 128:(i + 1) * 128, :]))

    def stage1(i):
        b_ = i % 2
        xt, junk, xn, ss, rs, zqk = xb[i % 3], junk_[b_], xn_[b_], ss_[b_], rs_[b_], zqk_[i % 3]
        if i + 2 < NTA:
            load_x(i + 2)
        ph.act(junk, xt, AF.Square, accum_out=ss)
        ph.rstd(rs, ss, float(DM), mhalf[:, 0:1], mode="sqrt")
        ph.act(xn, xt, AF.Copy, scale=rs)
        pT = PA.bitcast(BF16)
        for kt in range(8):
            ph.transpose(pT[:, kt * 128:(kt + 1) * 128], xn[:, kt * 128:(kt + 1) * 128], identb)
        xT = xnT[b_]
        ph.copy("dve", xT.re("p k t -> p (k t)"), pT)
        for ci, (c0, cw) in enumerate(((0, 512), (512, 512), (1024, 256))):
            for kt in range(8):
                ph.matmul(Z[ci][:, :cw], xT[:, kt, :], win[:, kt, c0:c0 + cw], start=(kt == 0), stop=False)
            ph.matmul(Z[ci][:, :cw], e0, bz[:, c0:c0 + cw], start=False, stop=True)
        ph.copy("act", zqk[:, 0:512], Z[0])
        ph.copy("act", zqk[:, 512:640], Z[1][:, 0:128])
        ph.copy("dve", Vx[:, i, :, 0:64], Z[1][:, 128:256].re("p (g d) -> p g d", g=2))
        u_ = ub[b_]
        ph.copy("act", u_[:, 0:256], Z[1][:, 256:512])
        ph.copy("act", u_[:, 256:512], Z[2][:, 0:256])
        ph.dma("sp", dT(D["u_scr"][i * 128:(i + 1) * 128, :]), u_)
        if dbg:
            zf = zfb
            ph.copy("dve", zf[:, 0:640], zqk)
            ph.copy("dve", zf[:, 640:1024], Z[1][:, 128:512])
            ph.copy("dve", zf[:, 1024:1280], Z[2][:, 0:256])
            ph.dma("sp", dT(D["dbg_z"][i * 128:(i + 1) * 128, :]), zf)

    def stage2(i):
        b_ = i % 2
        zqk, sq, qn, ssh, rh = zqk_[i % 3], sq_[b_], qn_[b_], ssh_[b_], rh_[b_]
        r1, r2, r3, r4, qr = r1_[b_], r2_[b_], r3_[b_], r4_[b_], qr_[b_]
        ph.tt("dve", sq, zqk, zqk, ALU.mult)
        ph.reduce(ssh, sq.re("p (h d) -> p h d", d=64))
        ph.rstd(rh, ssh, 64.0, mhalf[:, 0:10], mode="sqrt")
        ph.tt("dve", qn.re("p (h d) -> p h d", d=64), zqk.re("p (h d) -> p h d", d=64), rh.bc(2, [128, 10, 64]), ALU.mult)
        ph.tt("dve", qn, qn, gqk, ALU.mult)
        q4 = qn.re("p (h j two) -> p h j two", h=10, two=2)
        x1 = q4[:, :, :, 0]; x2 = q4[:, :, :, 1]
        csb = cosT[:, i, :].bc(1, [128, 10, 32]); snb = sinT[:, i, :].bc(1, [128, 10, 32])
        o4 = qr.re("p (h j two) -> p h j two", h=10, two=2)
        ph.tt("dve", r1, x1, csb, ALU.mult)
        ph.tt("dve", r2, x2, snb, ALU.mult)
        ph.tt("dve", o4[:, :, :, 0], r1, r2, ALU.subtract)
        ph.tt("dve", r3, x1, snb, ALU.mult)
        ph.tt("dve", r4, x2, csb, ALU.mult)
        ph.tt("dve", o4[:, :, :, 1], r3, r4, ALU.add)
        pq = PB.bitcast(BF16)
        for j in range(4):
            ph.transpose(pq[:, j * 128:(j + 1) * 128], qr[:, j * 128:(j + 1) * 128], identb)
        ph.transpose(pq[:, 512:640], qr[:, 512:640], identb)
        ph.copy("dve", qT[:, :, i * 128:(i + 1) * 128], pq[:, 0:512].re("p (j t) -> p j t", j=4))
        ph.copy("act", kTp[0][0:64, i * 128:(i + 1) * 128], pq[0:64, 512:640])
        ph.copy("act", kTp[1][64:128, i * 128:(i + 1) * 128], pq[64:128, 512:640])

    for i in range(min(2, NTA)):
        load_x(i)
    for i in range(NTA):
        stage1(i)
        if i >= 2:
            stage2(i - 2)
    for i in range(max(0, NTA - 2), NTA):
        stage2(i)

    PT = [ph.sb([128, 2, 4, 128], BF16, name=f"PT{i}") for i in range(3)]
    ot = two([65, 512], F32, "ot")
    oat = two([128, 512], F32, "oat")
    oan = two([128, 512], BF16, "oan")
    mT = two([128, 4, 128], BF16, "mT")
    rec = two([128, 8], F32, "rec")
    ss2 = two([128, 1], F32, "ss2"); rs2 = two([128, 1], F32, "rs2")
    junk2 = two([128, 512], BF16, "junk2")
    steps = [(i, gq, kg) for i in range(ATT) for gq in range(2) for kg in range(16)]
    deferred = []

    def qk(n):
        i, gq, kg = steps[n]
        pr = slice(64 * gq, 64 * gq + 64)
        Sx = S[n % 3]
        for kk in range(2):
            kt = 2 * kg + kk
            ph.matmul(Sx[:, kk * 512:(kk + 1) * 512], kTp[gq][:, kt * 128:(kt + 1) * 128],
                      qT[:, :, i * 128:(i + 1) * 128], start=True, stop=True)

    def evac(i, gq):
        b = i % 2
        c = (2 * i + gq) % 2
        pe_ = PB[:, 0:260].re("p (j d) -> p j d", j=4)
        for j in range(4):
            ph.transpose(pe_[:, j, :], ot[c][0:65, j * 128:(j + 1) * 128], identf[0:65, 0:65])
        ph.recip(rec[b][:, 4 * gq:4 * gq + 4], pe_[:, :, 64])
        ph.tt("dve", oat[b][:, 256 * gq:256 * gq + 256].re("p (j d) -> p j d", j=4), pe_[:, :, 0:64],
              rec[b][:, 4 * gq:4 * gq + 4].bc(2, [128, 4, 64]), ALU.mult)
        if gq == 1:
            ph.act(junk2[b], oat[b], AF.Square, accum_out=ss2[b])
            ph.rstd(rs2[b], ss2[b], 512.0, mhalf[:, 0:1])
            ph.act(oan[b], oat[b], AF.Copy, scale=rs2[b])

    def finalize(i):
        b = i % 2
        pmx = PB.bitcast(BF16)[:, 0:512]
        for k in range(4):
            ph.transpose(pmx[:, k * 128:(k + 1) * 128], oan[b][:, k * 128:(k + 1) * 128], identb)
        ph.copy("dve", mT[b].re("p k t -> p (k t)"), pmx)
        ph.dma("sp", dT(D["mixa_scr"][:, :, i * 128:(i + 1) * 128]), mT[b])

    def pv(n):
        i, gq, kg = steps[n]
        PTx = PT[n % 3]
        c = (2 * i + gq) % 2
        acc = PA[0:65, :]
        for kk in range(2):
            kt = 2 * kg + kk
            ph.matmul(acc, Vx[:, kt, gq, :], PTx[:, kk, :, :], start=(kt == 0), stop=(kt == NT - 1))
        if kg == 15:
            ph.copy("dve", ot[c], acc)
            deferred.append((n + 2, "e", i, gq))
            if gq == 1:
                deferred.append((n + 4, "f", i, gq))

    def run_deferred(n):
        k = 0
        while k < len(deferred):
            if deferred[k][0] <= n:
                _, kind, i, gq = deferred.pop(k)
                if kind == "e":
                    evac(i, gq)
                else:
                    finalize(i)
            else:
                k += 1

    for n in range(min(2, len(steps))):
        qk(n)
    for n in range(len(steps)):
        if n + 2 < len(steps):
            qk(n + 2)
        ph.act(PT[n % 3].re("p a b c -> p (a b c)"), S[n % 3], AF.Exp, scale=0.125)
        pv(n)
        run_deferred(n)
    run_deferred(10 ** 9)
    ph.finish()


def phase_c(nc, l, G, D):
    ph = Phase(nc, f"C{l}")
    g = wrapG(ph, G)
    identb = g["identb"]
    DY = ph.sb([128, 2, 8192], BF16)
    us = D["u_scr"].rearrange("(kc k s) c -> kc k (s c)", kc=2, k=128, s=16)
    Uall = ph.sb([128, 32, 2, 256], BF16)
    XH = ph.sb([128, 32, 257, 2], F32)
    Hbf = ph.sb([128, 32, 257, 2], BF16)
    raw = Hbf.re("p g k r -> p (g k r)")
    for kc in range(2):
        rk = raw[:, kc * 8192:(kc + 1) * 8192]
        ph.dma("sp", rk, dT(us[kc]))
        ph.copy("dve", DY[:, kc, :].re("p (g s c) -> p g s c", g=32, s=16),
                rk.re("p (s g c) -> p g s c", s=16, g=32))
    dec = ph.sb([128, 2, 32, 2], F32)
    ph.dma("sp", dec, dT(D["dec_s"][l].rearrange("p (a g r) -> p a g r", a=2, r=2)))
    ph.memset("pool", XH[:, :, 0, :], 0.0)
    mt = [ph.sb([128, 1536], BF16, name=f"mt{i}") for i in range(3)]
    PS = [ph.ps([128, 512], F32, name=f"PS{i}") for i in range(8)]
    DYv = [DY[:, kc, :].re("p (s c) -> p s c", c=512) for kc in range(2)]

    def load_m(gg):
        ph.dma("sp", mt[gg % 3], dT(D["ssm_s"][l][gg]))

    load_m(0)
    load_m(1)
    for gg in range(32):
        if gg + 2 < 32:
            load_m(gg + 2)
        m = mt[gg % 3]
        msum = m[:, 512:1024].re("p (sh ri q) -> p sh ri q", sh=2, ri=2)
        pu = PS[gg % 2].bitcast(BF16)
        for sh in range(2):
            for kc in range(2):
                ph.transpose(pu[:, sh * 256 + kc * 128: sh * 256 + (kc + 1) * 128],
                             DY[:, kc, gg * 256 + sh * 128: gg * 256 + (sh + 1) * 128], identb)
        ph.copy("act" if gg % 2 == 0 else "pool", Uall[:, gg, :, :].re("p s k -> p (s k)"), pu[:, 0:512]) if False else \
            ph.copy("act", Uall[:, gg, :, :].re("p s k -> p (s k)"), pu[:, 0:512])
        px = PS[2 + gg % 2]
        pxv = px.re("p (ri k) -> p ri k", ri=2)
        for ri in range(2):
            for sh in range(2):
                ph.matmul(pxv[:, ri, :], msum[:, sh, ri, :], Uall[:, gg, sh, :], start=(sh == 0), stop=(sh == 1))
        ph.copy("dve", XH[0:64, gg, 1:257, :], pxv[0:64].re("p ri k -> p k ri"))
        pxr = T(pxv.ap[64:128, :, ::-1], pxv.buf)
        ph.copy("dve", XH[64:128, gg, 1:257, :], pxr.re("p ri k -> p k ri"))
    R = ph.sb([128, 32, 2], F32); Pm = ph.sb([128, 32, 2], F32); Qm = ph.sb([128, 32, 2], F32)
    Rsw = T(R.ap[:, :, ::-1], R.buf)
    AA = dec[:, 0]; AB = dec[:, 1]
    for k in range(256):
        ph.tt("dve", R, XH[:, :, k, :], XH[:, :, k + 1, :], ALU.add)
        ph.tt("dve", Pm, AA, R, ALU.mult)
        ph.tt("dve", Qm, AB, Rsw, ALU.mult)
        ph.tt("dve", XH[:, :, k + 1, :], Pm, Qm, ALU.add)
    for q in range(4):
        gsl = slice(8 * q, 8 * q + 8)
        ph.copy("act", Hbf[0:64, gsl, 0:256, :], XH[0:64, gsl, 0:256, :])
        xr = T(XH.ap[64:128, gsl, 255::-1, :], XH.buf)
        ph.copy("dve", Hbf[64:128, gsl, 0:256, :], xr)
    load_m(0)
    load_m(1)
    x2 = [ph.sb([128, 512], F32, name=f"x2_{i}") for i in range(2)]
    ygb = [ph.sb([128, 2, 256], BF16, name=f"ygb{i}") for i in range(2)]
    for gg in range(32):
        if gg + 2 < 32:
            load_m(gg + 2)
        m = mt[gg % 3]
        mintra = m[:, 0:512].re("p (sh q) -> p sh q", sh=2)
        mout = m[:, 1024:1536].re("p (ri q) -> p ri q", ri=2)
        py = PS[4 + gg % 2]
        pyv = py.re("p (th k) -> p th k", th=2)
        for th in range(2):
            for sh in range(2):
                ph.matmul(pyv[:, th, :], mintra[:, sh, th * 128:(th + 1) * 128], Uall[:, gg, sh, :], start=(sh == 0), stop=False)
            for ri in range(2):
                ph.matmul(pyv[:, th, :], mout[:, ri, th * 128:(th + 1) * 128], Hbf[:, gg, 0:256, ri], start=False, stop=(ri == 1))
        a = x2[gg % 2]
        yb = ygb[gg % 2]
        ph.act(a, py, AF.Square)
        ph.ts("dve", a, a, 0.044715, ALU.mult, 1.0, ALU.add)
        ph.tt("dve", a, a, py, ALU.mult)
        ph.act(a, a, AF.Tanh, scale=0.7978845608028654)
        ph.ts("dve", a, a, 0.5, ALU.mult, 0.5, ALU.add)
        ph.tt("dve", yb.re("p th k -> p (th k)"), a, py, ALU.mult)
        ptb = PS[6 + gg % 2].bitcast(BF16)
        for th in range(2):
            for kc in range(2):
                ph.transpose(ptb[:, (th * 2 + kc) * 128:(th * 2 + kc + 1) * 128], yb[:, th, kc * 128:(kc + 1) * 128], identb)
        ptv = ptb[:, 0:512].re("p (th kc t8 c) -> p th kc t8 c", th=2, kc=2, t8=8)
        for kc in range(2):
            outv = DYv[kc][:, :, 16 * gg:16 * gg + 16].re("p (th t8) c -> p th t8 c", th=2)
            ph.copy("act" if kc == 0 else "dve", outv, ptv[:, :, kc, :, :])
    ys = D["yg_scr"].rearrange("(kc k s) c -> kc k (s c)", kc=2, k=128, s=16)
    for kc in range(2):
        ph.dma("sp", dT(ys[kc]), DY[:, kc, :])
    ph.finish()


def phase_d1(nc, l, G, D):
    ph = Phase(nc, f"D{l}")
    g = wrapG(ph, G)
    identb, mhalf, e0 = g["identb"], g["mhalf"], g["e0"]
    xsrc = D["x"] if l == 0 else D["xcur"]
    wglu = ph.sb([128, 4, 512], BF16)
    ph.dma("sp", wglu, dT(D["wglu_s"][l].rearrange("(kt p) n -> p kt n", p=128)))
    wout = ph.sb([128, 8, DM], BF16)
    ph.dma("sp", wout, dT(D["wout_s"][l].rearrange("(kt p) n -> p kt n", p=128)))
    bgf = ph.sb([1, 512], F32); bgl = ph.sb([128, 512], BF16)
    ph.memset("pool", bgl, 0.0)
    ph.dma("sp", bgf, dT(D["b_glu_row"][l]))
    ph.copy("dve", bgl[0:1, :], bgf)
    PS = [ph.ps([128, 512], F32, name=f"PS{i}") for i in range(8)]

    def two(shape, dt, nm, n=2):
        return [ph.sb(shape, dt, name=f"{nm}{i}") for i in range(n)]
    ygt = two([128, 512], BF16, "ygt", 3); mTa = two([128, 4, 128], BF16, "mTa", 3); xb = two([128, DM], F32, "xb", 3)
    ygT = two([128, 4, 128], BF16, "ygT"); mTs = two([128, 4, 128], BF16, "mTs"); xo = two([128, DM], F32, "xo")
    sg_ = two([128, 512], F32, "sg"); o_ = two([128, 512], F32, "o"); on_ = two([128, 512], BF16, "on")
    junk_ = two([128, 512], BF16, "junk"); ss_ = two([128, 1], F32, "ss"); rs_ = two([128, 1], F32, "rs")

    def load(i):
        rows = slice(i * 128, (i + 1) * 128)
        ph.dma("sp", ygt[i % 3], dT(D["yg_scr"][rows, :]))
        ph.dma("sp", mTa[i % 3], dT(D["mixa_scr"][:, :, rows]))
        ph.dma("sp", xb[i % 3], dT(xsrc[rows, :]))

    def stage1(i):
        b = i % 2
        yg = ygt[i % 3]
        sg, o, on, junk, ss, rs = sg_[b], o_[b], on_[b], junk_[b], ss_[b], rs_[b]
        pt = PS[0 + b].bitcast(BF16)
        for k in range(4):
            ph.transpose(pt[:, k * 128:(k + 1) * 128], yg[:, k * 128:(k + 1) * 128], identb)
        ph.copy("dve", ygT[b].re("p k t -> p (k t)"), pt[:, 0:512])
        pg = PS[2 + b]
        for k in range(4):
            ph.matmul(pg, ygT[b][:, k, :], wglu[:, k, :], start=(k == 0), stop=False)
        ph.matmul(pg, e0, bgl, start=False, stop=True)
        ph.act(sg, pg, AF.Tanh, scale=0.5)
        ph.ts("dve", sg, sg, 0.5, ALU.mult, 0.5, ALU.add)
        ph.tt("dve", o, sg, yg, ALU.mult)
        ph.act(junk, o, AF.Square, accum_out=ss)
        ph.rstd(rs, ss, 512.0, mhalf[:, 0:1])
        ph.act(on, o, AF.Copy, scale=rs)

    def stage2(i):
        b = i % 2
        rows = slice(i * 128, (i + 1) * 128)
        pt2 = PS[4 + b].bitcast(BF16)
        for k in range(4):
            ph.transpose(pt2[:, k * 128:(k + 1) * 128], on_[b][:, k * 128:(k + 1) * 128], identb)
        ph.copy("dve", mTs[b].re("p k t -> p (k t)"), pt2[:, 0:512])
        for c2 in range(2):
            po = PS[6 + c2]
            for k in range(8):
                lhsT = mTa[i % 3][:, k, :] if k < 4 else mTs[b][:, k - 4, :]
                ph.matmul(po, lhsT, wout[:, k, c2 * 512:(c2 + 1) * 512], start=(k == 0), stop=(k == 7))
            ph.tt("dve", xo[b][:, c2 * 512:(c2 + 1) * 512], po, xb[i % 3][:, c2 * 512:(c2 + 1) * 512], ALU.add)
        ph.dma("sp", dT(D["xmid"][rows, :]), xo[b])

    load(0)
    load(1)
    for i in range(NT):
        stage1(i)
        if i >= 1:
            stage2(i - 1)
        if i + 2 < NT:
            load(i + 2)
    stage2(NT - 1)
    ph.finish()


def phase_d2(nc, l, G, D, last):
    ph = Phase(nc, f"F{l}")
    g = wrapG(ph, G)
    identb, mhalf = g["identb"], g["mhalf"]
    wfi = ph.sb([128, 8, 2 * FH], BF16)
    wfiv = D["wfi_s"][l].rearrange("(kt p) n -> p kt n", p=128)
    for kt in range(8):
        ph.dma("sp", wfi[:, kt, :], dT(wfiv[:, kt, :]))
    wfo = ph.sb([128, 22, DM], BF16)
    wfov = D["wfo_s"][l].rearrange("(ft p) n -> p ft n", p=128)
    for f0 in range(0, 22, 6):
        nf = min(6, 22 - f0)
        ph.dma("sp", wfo[:, f0:f0 + nf, :], dT(wfov[:, f0:f0 + nf, :]))
    bfi = ph.sb([128, 44], F32); hbg = ph.sb([128, 22], F32)
    ph.dma("sp", bfi, dT(D["bfi_s"][l]))
    ph.ts("dve", hbg, bfi[:, 0:22], 0.5, ALU.mult)
    if last:
        fnb = ph.sb([128, DM], F32)
        ph.dma("sp", fnb, dT(D["fnorm_row"].broadcast_to([128, DM])))
    PS = [ph.ps([128, 512], F32, name=f"PS{i}") for i in range(8)]
    TB = 2
    xb = [ph.sb([128, DM], F32, name=f"xb{i}") for i in range(2 * TB)]
    xo = [ph.sb([128, DM], F32, name=f"xo{i}") for i in range(2)]
    xn = ph.sb([128, DM], BF16); junk = ph.sb([128, DM], BF16)
    xnT = [ph.sb([128, 8, TB * 128], BF16, name=f"xnT{i}") for i in range(2)]
    actT = [ph.sb([128, 22, TB * 128], BF16, name=f"actT{i}") for i in range(2)]
    ss = ph.sb([128, 1], F32); rs = ph.sb([128, 1], F32)
    thb = [ph.sb([128, TB * 128], F32, name=f"thb{i}") for i in range(2)]
    gbb = [ph.sb([128, TB * 128], F32, name=f"gbb{i}") for i in range(2)]
    sb_ = [ph.sb([128, TB * 128], F32, name=f"sb{i}") for i in range(2)]
    dst = D["out"] if last else D["xcur"]
    W = TB * 128
    nmm = 0
    def load_blk(blk):
        for tt in range(TB):
            i = blk * TB + tt
            ph.dma("sp", xb[(blk % 2) * TB + tt], dT(D["xmid"][i * 128:(i + 1) * 128, :]))

    def prep(blk):
        bb = blk % 2
        for tt in range(TB):
            xt = xb[bb * TB + tt]
            ph.act(junk, xt, AF.Square, accum_out=ss)
            ph.rstd(rs, ss, float(DM), mhalf[:, 0:1])
            ph.act(xn, xt, AF.Copy, scale=rs)
            pT = PS[0].bitcast(BF16)
            for kt in range(8):
                ph.transpose(pT[:, kt * 128:(kt + 1) * 128], xn[:, kt * 128:(kt + 1) * 128], identb)
            ph.copy("dve", xnT[bb][:, :, tt * 128:(tt + 1) * 128], pT.re("p (k t) -> p k t", k=8))

    load_blk(0)
    prep(0)
    for blk in range(NT // TB):
        bb = blk % 2
        if blk + 1 < NT // TB:
            load_blk(blk + 1)
        for f in range(22):
            pgu = PS[1 + nmm % 2]
            nmm += 1
            pg = pgu[:, 0:W]; pu = pgu[:, W:2 * W]
            for kt in range(8):
                ph.matmul(pg, wfi[:, kt, f * 128:(f + 1) * 128], xnT[bb][:, kt, :], start=(kt == 0), stop=(kt == 7))
            for kt in range(8):
                ph.matmul(pu, wfi[:, kt, FH + f * 128:FH + (f + 1) * 128], xnT[bb][:, kt, :], start=(kt == 0), stop=(kt == 7))
            t_ = thb[f % 2]; g_ = gbb[f % 2]; s_ = sb_[f % 2]
            ph.act(t_, pg, AF.Tanh, scale=0.5, bias=hbg[:, f:f + 1])
            ph.ts("pool", g_, pg, bfi[:, f:f + 1], ALU.add) if False else ph.act(g_, pg, AF.Identity, bias=bfi[:, f:f + 1])
            ph.stt(s_, t_, 1.0, g_, ALU.add, ALU.mult)
            ph.stt(actT[bb][:, f, :], pu, bfi[:, 22 + f:23 + f], s_, ALU.add, ALU.mult)
        if blk + 1 < NT // TB:
            prep(blk + 1)
        for tt in range(TB):
            i = blk * TB + tt
            xt = xb[bb * TB + tt]
            xo_ = xo[tt % 2]
            for c2 in range(2):
                py = PS[3 + 2 * (tt % 2) + c2]
                for f in range(22):
                    ph.matmul(py, actT[bb][:, f, tt * 128:(tt + 1) * 128], wfo[:, f, c2 * 512:(c2 + 1) * 512],
                              start=(f == 0), stop=(f == 21))
                ph.tt("dve", xo_[:, c2 * 512:(c2 + 1) * 512], py, xt[:, c2 * 512:(c2 + 1) * 512], ALU.add)
            if last:
                ph.act(junk, xo_, AF.Square, accum_out=ss)
                ph.rstd(rs, ss, float(DM), mhalf[:, 0:1])
                ph.stt(xo_, xo_, rs, fnb, ALU.mult, ALU.mult)
            ph.dma("sp", dT(dst[i * 128:(i + 1) * 128, :]), xo_)
    ph.finish()


QPERM = np.concatenate([np.arange(64) + 64 * (g * 4 + j) for j in range(4) for g in range(2)] + [np.arange(512, 1280)])


def _consts():
    ident = np.eye(128, dtype=np.float32)
    p = np.arange(128)
    s8 = p // 16
    col = np.arange(256) // 16
    maskf = np.zeros((128, 2, 256), np.float32)
    maskb = np.zeros((128, 2, 256), np.float32)
    for sh in range(2):
        s = s8 + 8 * sh
        maskf[:, sh, :] = (col[None, :] >= s[:, None])
        maskb[:, sh, :] = (col[None, :] <= s[:, None])
    t = np.arange(SEQ)
    inv = (1.0 / (10000.0 ** (np.arange(0, 32, 2, dtype=np.float32) / 32.0))).astype(np.float32)
    row = (t // 64).astype(np.float32); cc = (t % 64).astype(np.float32)
    ang = np.concatenate([row[:, None] * inv[None, :], cc[:, None] * inv[None, :]], axis=-1).astype(np.float32)
    cosv = np.cos(ang).astype(np.float32); sinv = np.sin(ang).astype(np.float32)
    ropec = cosv.reshape(NT, 128, 32).transpose(1, 0, 2).reshape(128, NT * 32)
    ropes = sinv.reshape(NT, 128, 32).transpose(1, 0, 2).reshape(128, NT * 32)
    mexp = np.zeros((128, 34), np.float32)
    m = np.arange(16, dtype=np.float32)
    mexp[:64, 0:16] = -m; mexp[64:, 0:16] = m
    mexp[:64, 16:32] = m; mexp[64:, 16:32] = -m
    mexp[:, 32] = 16.0; mexp[:, 33] = 1.0
    return dict(ident=ident, maskf=maskf.reshape(128, 512), maskb=maskb.reshape(128, 512),
                ropec=np.ascontiguousarray(ropec), ropes=np.ascontiguousarray(ropes), mexp=mexp)


def _layout(inp):
    L = 4
    f = lambda a: np.ascontiguousarray(a, dtype=np.float32)
    col = lambda v: f(v.reshape(L, -1, 128).transpose(0, 2, 1))
    sh = {}
    sh["w_ada"] = f(inp["w_ada"]); sh["b_ada_row"] = f(inp["b_ada"].reshape(L, 1, 6144))
    sh["norm1_col"] = col(inp["norm1"]); sh["norm2_col"] = col(inp["norm2"])
    sh["mixn_col"] = col(np.concatenate([inp["attn_out_norm"], inp["ssm_out_norm"]], axis=-1))
    for k in ("w_out", "w_ffn_in", "w_ffn_out", "w_glu"):
        sh[k] = f(inp[k])
    sh["w_in"] = f(inp["w_in"][:, :, QPERM])
    sh["b_glu_row"] = f(inp["b_glu"].reshape(L, 1, 512))
    sh["gain_qk"] = f(np.concatenate([np.tile(inp["q_norm"], (1, 8)), np.tile(inp["k_norm"], (1, 2))], axis=-1).reshape(L, 1, 640))
    sh["fnorm_row"] = f(inp["final_norm"].reshape(1, DM))
    dp = lambda a: f(a.transpose(0, 1, 3, 2).reshape(L, 128, 32))
    sh["lam_re"] = dp(inp["ssm_lam_re"]); sh["lam_im"] = dp(inp["ssm_lam_im"])
    sh["log_dt"] = f(np.repeat(inp["ssm_log_dt"][:, :, None, :], 64, axis=2).reshape(L, 128, 32))
    bl = lambda a: f(a.transpose(0, 1, 3, 2, 4).reshape(L, 128, 512))
    cl = lambda a: f(a.transpose(0, 1, 4, 2, 3).reshape(L, 128, 512))
    sh["b_re"] = bl(inp["ssm_b_re"]); sh["b_im"] = bl(inp["ssm_b_im"])
    sh["c_re"] = cl(inp["ssm_c_re"]); sh["c_im"] = cl(inp["ssm_c_im"])
    dv = inp["ssm_d"].reshape(L, 32, 16)
    sh["dvec"] = f(np.tile(dv.transpose(0, 2, 1)[:, None, :, :], (1, 8, 1, 1)).reshape(L, 128, 32))
    sh.update(_consts())
    return sh


_CACHE = {}


def kernel(**inputs):
    inp = {k: np.asarray(v) for k, v in inputs.items()}
    shared = _layout(inp)
    if "nc" not in _CACHE:
        _CACHE["nc"] = build(4, False)
    nc = _CACHE["nc"]
    in_maps = []
    for b in range(8):
        m = dict(shared)
        m["x"] = np.ascontiguousarray(inp["x"][b], dtype=np.float32)
        m["cT"] = np.ascontiguousarray(inp["c"][b].reshape(8, 128).T, dtype=np.float32)
        in_maps.append(m)
    res = run_bass_kernel_spmd(nc, in_maps, core_ids=list(range(8)))
    out = np.stack([np.asarray(r["out"], dtype=np.float32) for r in res.results], axis=0)
    return out
```

```python
import contextlib
import math
import numpy as np
import concourse.bass as bass
import concourse.mybir as mybir
from concourse.bass_utils import run_bass_kernel_spmd

F32 = mybir.dt.float32
BF16 = mybir.dt.bfloat16
I32 = mybir.dt.int32
AF = mybir.ActivationFunctionType
ALU = mybir.AluOpType
AX = mybir.AxisListType

ENGS = ("pe", "act", "dve", "pool", "sp")
SEQ = 4096
DM = 1024
NT = SEQ // 128
FH = 2816
TWO_PI = 2.0 * math.pi


class Buf:
    __slots__ = ("name", "w", "r", "excl")

    def __init__(self, name="", excl=False):
        self.name = name
        self.w = None
        self.r = {}
        self.excl = excl


class T:
    __slots__ = ("ap", "buf")

    def __init__(self, ap, buf):
        self.ap = ap
        self.buf = buf

    def __getitem__(self, idx):
        return T(self.ap[idx], self.buf)

    def re(self, pattern, **kw):
        return T(self.ap.rearrange(pattern, **kw), self.buf)

    def bc(self, axis, shape):
        return T(self.ap.unsqueeze(axis).to_broadcast(list(shape)), self.buf)

    def bitcast(self, dt):
        return T(self.ap.bitcast(dt), self.buf)


def dT(ap):
    return T(ap, Buf("dram"))


class Phase:
    def __init__(self, nc, name, n_dma_sems=24, same_engine_sync=True):
        self.nc = nc
        self.name = name
        self.ses = same_engine_sync
        self.stack = contextlib.ExitStack()
        self.ops = {e: [] for e in ENGS}
        self.cnt = {e: 0 for e in ENGS}
        self.seen = {e: {} for e in ENGS}
        self.sem = {e: self.stack.enter_context(nc.semaphore(f"{name}_{e}")) for e in ENGS}
        self.dsem = [self.stack.enter_context(nc.semaphore(f"{name}_d{i}")) for i in range(n_dma_sems)]
        self.dtot = [0] * n_dma_sems
        self.dnext = 0
        self.nbuf = 0
        self.rr = 0

    def sb(self, shape, dtype, name=None):
        self.nbuf += 1
        name = name or f"t{self.nbuf}"
        t = self.stack.enter_context(self.nc.sbuf_tensor(f"{self.name}_{name}", list(shape), dtype))
        return T(t[:], Buf(name))

    def ps(self, shape, dtype=F32, name=None):
        self.nbuf += 1
        name = name or f"p{self.nbuf}"
        t = self.stack.enter_context(self.nc.psum_tensor(f"{self.name}_{name}", list(shape), dtype))
        tt_ = T(t[:], Buf(name, excl=True))
        self.memset("dve", tt_, 0.0)
        return tt_

    def wrap(self, ap):
        return T(ap, Buf("g"))

    def _wait(self, eng, tok):
        key, sem, val = tok
        if key == eng and (eng == "pe" or not self.ses):
            return
        if self.seen[eng].get(key, 0) >= val:
            return
        self.seen[eng][key] = val
        self.ops[eng].append(lambda e, s=sem, v=val: e.wait_ge(s, v))

    def _deps(self, eng, reads, writes):
        for b in reads:
            if b.w is not None:
                self._wait(eng, b.w)
        for b in writes:
            if b.w is not None:
                self._wait(eng, b.w)
            for tok in b.r.values():
                self._wait(eng, tok)

    def _update(self, tok, reads, writes):
        for b in reads:
            b.r[tok[0]] = tok
        for b in writes:
            b.w = tok
            b.r = {}

    def issue(self, eng, fn, reads, writes):
        reads = [x.buf if isinstance(x, T) else x for x in reads]
        writes = [x.buf if isinstance(x, T) else x for x in writes]
        writes = writes + [b for b in reads if b.excl and b not in writes]
        self._deps(eng, reads, writes)
        self.cnt[eng] += 1
        tok = (eng, self.sem[eng], self.cnt[eng])
        s = self.sem[eng]
        self.ops[eng].append(lambda e, fn=fn, s=s: fn(e).then_inc(s, 1))
        self._update(tok, reads, writes)

    def dma(self, eng, out, in_, **kw):
        reads = [in_.buf]
        writes = [out.buf]
        self._deps(eng, reads, writes)
        j = self.dnext
        self.dnext = (self.dnext + 1) % len(self.dsem)
        key = f"d{j}"
        if self.dtot[j] > 0:
            self._wait(eng, (key, self.dsem[j], self.dtot[j]))
        self.dtot[j] += 16
        tok = (key, self.dsem[j], self.dtot[j])
        s = self.dsem[j]
        oa, ia = out.ap, in_.ap
        self.ops[eng].append(lambda e, oa=oa, ia=ia, s=s, kw=kw: e.dma_start(out=oa, in_=ia, **kw).then_inc(s, 16))
        self._update(tok, reads, writes)

    def finish(self):
        for j, tot in enumerate(self.dtot):
            if tot > 0:
                self._wait("sp", (f"d{j}", self.dsem[j], tot))
        nc = self.nc
        ops = self.ops
        sems = list(self.sem.values()) + list(self.dsem)
        with nc.Block() as b0:
            @b0.sync
            def _(e):
                for s_ in sems:
                    e.sem_clear(s_)
        with nc.Block() as block:
            @block.tensor
            def _(e):
                for f in ops["pe"]:
                    f(e)

            @block.scalar
            def _(e):
                for f in ops["act"]:
                    f(e)

            @block.vector
            def _(e):
                for f in ops["dve"]:
                    f(e)

            @block.gpsimd
            def _(e):
                for f in ops["pool"]:
                    f(e)

            @block.sync
            def _(e):
                for f in ops["sp"]:
                    f(e)
        self.stack.close()

    def matmul(self, out, lhsT, rhs, start=True, stop=True, **kw):
        self.issue("pe", lambda e: e.matmul(out.ap, lhsT.ap, rhs.ap, start=start, stop=stop, **kw),
                   [lhsT, rhs], [out])

    def transpose(self, out, in_, ident):
        self.issue("pe", lambda e: e.transpose(out.ap, in_.ap, ident.ap), [in_, ident], [out])

    def act(self, out, in_, func, scale=1.0, bias=None, accum_out=None):
        reads = [in_]
        writes = [out]
        kw = {}
        if isinstance(bias, T):
            reads.append(bias)
            kw["bias"] = bias.ap
        elif bias is not None:
            kw["bias"] = bias
        if isinstance(scale, T):
            reads.append(scale)
            kw["scale"] = scale.ap
        else:
            kw["scale"] = scale
        if accum_out is not None:
            writes.append(accum_out)
            kw["accum_out"] = accum_out.ap
        self.issue("act", lambda e: e.activation(out=out.ap, in_=in_.ap, func=func, **kw), reads, writes)

    def tt(self, eng, out, in0, in1, op):
        self.issue(eng, lambda e: e.tensor_tensor(out=out.ap, in0=in0.ap, in1=in1.ap, op=op), [in0, in1], [out])

    def ts(self, eng, out, in0, s1, op0, s2=None, op1=None):
        reads = [in0]
        a1 = s1.ap if isinstance(s1, T) else s1
        a2 = s2.ap if isinstance(s2, T) else s2
        if isinstance(s1, T):
            reads.append(s1)
        if isinstance(s2, T):
            reads.append(s2)
        kw = {}
        if op1 is not None:
            kw["op1"] = op1
        self.issue(eng, lambda e: e.tensor_scalar(out=out.ap, in0=in0.ap, scalar1=a1, scalar2=a2, op0=op0, **kw),
                   reads, [out])

    def stt(self, out, in0, scalar, in1, op0, op1):
        reads = [in0, in1]
        a = scalar.ap if isinstance(scalar, T) else scalar
        if isinstance(scalar, T):
            reads.append(scalar)
        self.issue("dve", lambda e: e.scalar_tensor_tensor(out=out.ap, in0=in0.ap, scalar=a, in1=in1.ap, op0=op0, op1=op1),
                   reads, [out])

    def copy(self, eng, out, in_):
        if eng == "act":
            self.issue(eng, lambda e: e.copy(out=out.ap, in_=in_.ap), [in_], [out])
        else:
            self.issue(eng, lambda e: e.tensor_copy(out=out.ap, in_=in_.ap), [in_], [out])

    def memset(self, eng, out, val):
        self.issue(eng, lambda e: e.memset(out.ap, val), [], [out])

    def reduce(self, out, in_, op=None, axis=None):
        op = op or ALU.add
        axis = axis or AX.X
        self.issue("dve", lambda e: e.tensor_reduce(out=out.ap, in_=in_.ap, axis=axis, op=op), [in_], [out])

    def recip(self, out, in_):
        self.issue("dve", lambda e: e.reciprocal(out=out.ap, in_=in_.ap), [in_], [out])

    def rstd(self, out, ss, n, mhalf, mode="sqrt"):
        if mode == "pow":
            self.ts("dve", out, ss, 1.0 / n, ALU.mult, 1e-6, ALU.add)
            self.tt("pool", out, out, mhalf, ALU.pow)
            return
        if mode == "sqrt":
            self.act(out, ss, AF.Sqrt, scale=1.0 / n, bias=self.epsT(ss))
            self.recip(out, out)
            return
        shp = list(ss.ap.shape)
        a = self.sb(shp, F32); y = self.sb(shp, F32); t = self.sb(shp, F32)
        self.ts("dve", a, ss, 1.0 / n, ALU.mult, 1e-6, ALU.add)
        ai = a.bitcast(I32); yi = y.bitcast(I32)
        self.issue("dve", lambda e: e.tensor_scalar(out=yi.ap, in0=ai.ap, scalar1=1, scalar2=None, op0=ALU.arith_shift_right), [a], [y])
        self.issue("dve", lambda e: e.tensor_scalar(out=yi.ap, in0=yi.ap, scalar1=-1, scalar2=None, op0=ALU.bitwise_xor), [y], [y])
        self.issue("dve", lambda e: e.tensor_scalar_add_int(yi.ap, yi.ap, 0x5f3759df + 1), [y], [y])
        for it in range(2):
            self.tt("dve", t, y, y, ALU.mult)
            self.tt("dve", t, t, a, ALU.mult)
            self.ts("dve", t, t, -0.5, ALU.mult, 1.5, ALU.add)
            self.tt("dve", out if it == 1 else y, y, t, ALU.mult)

    def epsT(self, like):
        if not hasattr(self, "_eps"):
            self._eps = self.sb([128, 1], F32, name="epsc")
            self.memset("pool", self._eps, 1e-6)
        return self._eps[0:like.ap.shape[0], :]

    def eng3(self):
        self.rr += 1
        return ("dve", "pool", "act")[self.rr % 3]

    def scale_rows(self, out, in_, col):
        e = self.eng3()
        if e == "act":
            self.act(out, in_, AF.Copy, scale=col)
        else:
            self.ts(e, out, in_, col, ALU.mult)


def build(NL=4, dbg=False):
    nc = bass.Bass("TRN2", target_bir_lowering=False)
    D = {}

    def din(name, shape, dt=F32):
        D[name] = nc.dram_tensor(name, list(shape), dt, kind="ExternalInput").ap()

    def dscr(name, shape, dt, out=False):
        D[name] = nc.dram_tensor(name, list(shape), dt, kind=("ExternalOutput" if out else "Internal")).ap()

    L = 4
    din("x", [SEQ, DM]); din("cT", [128, 8])
    din("w_ada", [L, DM, 6144]); din("b_ada_row", [L, 1, 6144])
    din("norm1_col", [L, 128, 8]); din("norm2_col", [L, 128, 8]); din("mixn_col", [L, 128, 8])
    din("w_in", [L, DM, 1280]); din("w_out", [L, DM, DM]); din("w_ffn_in", [L, DM, 2 * FH])
    din("w_ffn_out", [L, FH, DM]); din("w_glu", [L, 512, 512]); din("b_glu_row", [L, 1, 512])
    din("gain_qk", [L, 1, 640]); din("fnorm_row", [1, DM])
    din("lam_re", [L, 128, 32]); din("lam_im", [L, 128, 32]); din("log_dt", [L, 128, 32])
    din("b_re", [L, 128, 512]); din("b_im", [L, 128, 512]); din("c_re", [L, 128, 512]); din("c_im", [L, 128, 512])
    din("dvec", [L, 128, 32])
    din("ident", [128, 128]); din("maskf", [128, 512]); din("maskb", [128, 512])
    din("ropec", [128, NT * 32]); din("ropes", [128, NT * 32]); din("mexp", [128, 34])

    dscr("out", [SEQ, DM], F32, out=True)
    dscr("win_s", [L, DM, 1280], BF16); dscr("bz_s", [L, 1, 1280], BF16)
    dscr("wout_s", [L, DM, DM], BF16); dscr("wfi_s", [L, DM, 2 * FH], BF16); dscr("bfi_s", [L, 128, 44], F32)
    dscr("wfo_s", [L, FH, DM], BF16); dscr("wglu_s", [L, 512, 512], BF16)
    dscr("ssm_s", [L, 32, 128, 1536], BF16, out=dbg); dscr("dec_s", [L, 128, 128], F32, out=dbg)
    dscr("u_scr", [SEQ, 512], BF16, out=dbg); dscr("yg_scr", [SEQ, 512], BF16, out=dbg)
    dscr("mixa_scr", [128, 4, SEQ], BF16, out=dbg)
    dscr("xmid", [SEQ, DM], F32, out=dbg); dscr("xcur", [SEQ, DM], F32, out=dbg)
    if dbg:
        dscr("dbg_z", [SEQ, 1280], F32, out=True)

    gst = contextlib.ExitStack()

    def gsb(name, shape, dt):
        return gst.enter_context(nc.sbuf_tensor(name, list(shape), dt))[:]

    G = dict(identf=gsb("g_identf", [128, 128], F32), identb=gsb("g_identb", [128, 128], BF16),
             onesf=gsb("g_onesf", [1, 128], F32), onesb=gsb("g_onesb", [1, 128], BF16),
             cact=gsb("g_cact", [128, 8], F32), mhalf=gsb("g_mhalf", [128, 16], F32),
             e0=gsb("g_e0", [128, 128], BF16))

    ph = Phase(nc, "G0")
    identf, identb, onesf, onesb, cact, mhalf = (ph.wrap(G[k]) for k in ("identf", "identb", "onesf", "onesb", "cact", "mhalf"))
    ph.dma("sp", identf, dT(D["ident"]))
    ph.copy("dve", identb, identf)
    ph.memset("pool", onesf, 1.0)
    ph.memset("pool", onesb, 1.0)
    ph.memset("pool", mhalf, -0.5)
    e0 = ph.wrap(G["e0"])
    ph.memset("pool", e0, 0.0)
    ph.memset("pool", e0[0:1, :], 1.0)
    ct = ph.sb([128, 8], F32)
    th = ph.sb([128, 8], F32)
    ph.dma("sp", ct, dT(D["cT"]))
    ph.act(th, ct, AF.Tanh, scale=0.5)
    ph.ts("dve", th, th, 0.5, ALU.mult, 0.5, ALU.add)
    ph.tt("dve", cact, ct, th, ALU.mult)
    ph.finish()

    import os
    upto = os.environ.get("UPTO", "F")
    order = "STACDF"
    for l in range(NL):
        fns = [lambda: setup_layer(nc, l, G, D), lambda: setup_ssm(nc, l, G, D), lambda: phase_ab(nc, l, G, D, dbg),
               lambda: phase_c(nc, l, G, D), lambda: phase_d1(nc, l, G, D), lambda: phase_d2(nc, l, G, D, last=(l == NL - 1))]
        for ch, fn in zip(order, fns):
            if order.index(ch) <= order.index(upto):
                fn()
    gst.close()
    return nc


def wrapG(ph, G):
    return {k: ph.wrap(v) for k, v in G.items()}


def setup_layer(nc, l, G, D):
    ph = Phase(nc, f"S{l}")
    g = wrapG(ph, G)
    cact, onesf, identf, identb = g["cact"], g["onesf"], g["identf"], g["identb"]

    wa = D["w_ada"][l].rearrange("(kt p) n -> p kt n", p=128)
    modrow = ph.sb([1, 6144], F32)
    ph.dma("sp", modrow, dT(D["b_ada_row"][l]))
    wt = [ph.sb([128, 8, 512], F32, name=f"wt{i}") for i in range(2)]
    wb = [ph.sb([128, 8, 512], BF16, name=f"wb{i}") for i in range(2)]
    pm = [ph.ps([128, 512], F32, name=f"pm{i}") for i in range(4)]
    nld = [0]
    nst = [0]

    pend = []

    def flush():
        while pend:
            d_, s_ = pend.pop(0)
            ph.dma("sp", d_, s_)

    def ld(src_ap, cw, nk=8):
        t = wt[nld[0] % 2]
        nld[0] += 1
        ph.dma("sp", t[:, :nk, :cw], dT(src_ap))
        flush()
        return t

    def st(dst_ap, cw):
        t = wb[nst[0] % 2]
        nst[0] += 1
        return t

    for j in range(12):
        t = ld(wa[:, :, j * 512:(j + 1) * 512], 512)
        p = pm[j % 2]
        for kt in range(8):
            ph.matmul(p[0:1, :], cact[:, kt:kt + 1], t[:, kt, :], start=(kt == 0), stop=(kt == 7))
        ph.tt("dve", modrow[:, j * 512:(j + 1) * 512], p[0:1, :], modrow[:, j * 512:(j + 1) * 512], ALU.add)
    one11 = onesf[0:1, 0:1]
    pcol = pm[2]
    cols = list(range(0, 16)) + list(range(24, 40))
    for j in cols:
        ph.matmul(pcol[:, j:j + 1], modrow[0:1, j * 128:(j + 1) * 128], one11, start=True, stop=True)
    modcol = ph.sb([128, 48], F32)
    ph.copy("dve", modcol[:, 0:16], pcol[:, 0:16])
    ph.copy("dve", modcol[:, 24:40], pcol[:, 24:40])
    n1c = ph.sb([128, 8], F32); n2c = ph.sb([128, 8], F32); mixc = ph.sb([128, 8], F32)
    ph.dma("sp", n1c, dT(D["norm1_col"][l])); ph.dma("sp", n2c, dT(D["norm2_col"][l])); ph.dma("sp", mixc, dT(D["mixn_col"][l]))
    s1c = ph.sb([128, 8], F32); s2c = ph.sb([128, 8], F32)
    ph.stt(s1c, modcol[:, 8:16], 1.0, n1c, ALU.add, ALU.mult)
    ph.stt(s2c, modcol[:, 32:40], 1.0, n2c, ALU.add, ALU.mult)
    sh1c = modcol[:, 0:8]
    sh2c = modcol[:, 24:32]
    g1b = ph.sb([128, 1024], F32); g2b = ph.sb([128, 1024], F32)
    for (gb, off, sc) in ((g1b, 2048, 1.0), (g2b, 5120, 0.5)):
        for c2 in range(2):
            p = pm[c2]
            ph.matmul(p, onesf, modrow[0:1, off + c2 * 512: off + (c2 + 1) * 512], start=True, stop=True)
            ph.act(gb[:, c2 * 512:(c2 + 1) * 512], p, AF.Copy, scale=sc)

    wi = D["w_in"][l].rearrange("(kt p) n -> p kt n", p=128)
    wis = D["win_s"][l].rearrange("(kt p) n -> p kt n", p=128)
    bzrow = ph.sb([1, 1280], BF16)
    for ci, (c0, cw) in enumerate(((0, 512), (512, 512), (1024, 256))):
        t = ld(wi[:, :, c0:c0 + cw], cw)
        p = pm[ci % 2]
        for kt in range(8):
            ph.matmul(p[0:1, :cw], sh1c[:, kt:kt + 1], t[:, kt, :cw], start=(kt == 0), stop=(kt == 7))
        ph.copy("act", bzrow[:, c0:c0 + cw], p[0:1, :cw])
        o = st(None, cw)
        for kt in range(8):
            ph.scale_rows(o[:, kt, :cw], t[:, kt, :cw], s1c[:, kt:kt + 1])
        pend.append((dT(wis[:, :, c0:c0 + cw]), o[:, :, :cw]))
    ph.dma("sp", dT(D["bz_s"][l]), bzrow)

    wo = D["w_out"][l].rearrange("(kt p) n -> p kt n", p=128)
    wos = D["wout_s"][l].rearrange("(kt p) n -> p kt n", p=128)
    for c2 in range(2):
        t = ld(wo[:, :, c2 * 512:(c2 + 1) * 512], 512)
        o = st(None, 512)
        for kt in range(8):
            ph.stt(o[:, kt, :], t[:, kt, :], mixc[:, kt:kt + 1], g1b[:, c2 * 512:(c2 + 1) * 512], ALU.mult, ALU.mult)
        pend.append((dT(wos[:, :, c2 * 512:(c2 + 1) * 512]), o))

    wf = D["w_ffn_in"][l].rearrange("(kt p) n -> p kt n", p=128)
    wfs = D["wfi_s"][l].rearrange("(kt p) n -> p kt n", p=128)
    pbc = pm[3]
    for j in range(11):
        t = ld(wf[:, :, j * 512:(j + 1) * 512], 512)
        for f4 in range(4):
            for kt in range(8):
                ph.matmul(pbc[:, j * 4 + f4: j * 4 + f4 + 1], t[:, kt, f4 * 128:(f4 + 1) * 128], sh2c[:, kt:kt + 1],
                          start=(kt == 0), stop=(kt == 7))
        o = st(None, 512)
        for kt in range(8):
            ph.scale_rows(o[:, kt, :], t[:, kt, :], s2c[:, kt:kt + 1])
        pend.append((dT(wfs[:, :, j * 512:(j + 1) * 512]), o))
    bfi = ph.sb([128, 44], F32)
    ph.copy("dve", bfi, pbc[:, 0:44])
    ph.dma("sp", dT(D["bfi_s"][l]), bfi)

    wfo = D["w_ffn_out"][l].rearrange("(ft p) n -> p ft n", p=128)
    wfos = D["wfo_s"][l].rearrange("(ft p) n -> p ft n", p=128)
    for c2 in range(2):
        for f0 in range(0, 22, 8):
            nf = min(8, 22 - f0)
            t = ld(wfo[:, f0:f0 + nf, c2 * 512:(c2 + 1) * 512], 512, nf)
            o = st(None, 512)
            e = "dve" if (f0 // 8) % 2 == 0 else "pool"
            ph.tt(e, o[:, :nf, :], t[:, :nf, :], g2b[:, c2 * 512:(c2 + 1) * 512].bc(1, [128, nf, 512]), ALU.mult)
            pend.append((dT(wfos[:, f0:f0 + nf, c2 * 512:(c2 + 1) * 512]), o[:, :nf, :]))
    wg = D["w_glu"][l].rearrange("(kt p) n -> p kt n", p=128)
    wgs = D["wglu_s"][l].rearrange("(kt p) n -> p kt n", p=128)
    t = ld(wg, 512, 4)
    o = st(None, 512)
    ph.copy("pool", o[:, 0:4, :], t[:, 0:4, :])
    pend.append((dT(wgs), o[:, 0:4, :]))

    flush()
    ph.finish()


def setup_ssm(nc, l, G, D):
    ph = Phase(nc, f"T{l}")
    g = wrapG(ph, G)
    identf, identb = g["identf"], g["identb"]
    pm = [ph.ps([128, 512], F32, name=f"pm{i}") for i in range(4)]
    lre = ph.sb([128, 32], F32); lim = ph.sb([128, 32], F32); ldt = ph.sb([128, 32], F32)
    ph.dma("sp", lre, dT(D["lam_re"][l])); ph.dma("sp", lim, dT(D["lam_im"][l])); ph.dma("sp", ldt, dT(D["log_dt"][l]))
    mexp = ph.sb([128, 34], F32)
    ph.dma("sp", mexp, dT(D["mexp"]))
    bre = ph.sb([128, 32, 16], F32); bim = ph.sb([128, 32, 16], F32)
    cre = ph.sb([128, 32, 16], F32); cim = ph.sb([128, 32, 16], F32)
    ph.dma("sp", bre, dT(D["b_re"][l].rearrange("p (g c) -> p g c", c=16)))
    ph.dma("sp", bim, dT(D["b_im"][l].rearrange("p (g c) -> p g c", c=16)))
    ph.dma("sp", cre, dT(D["c_re"][l].rearrange("p (g c) -> p g c", c=16)))
    ph.dma("sp", cim, dT(D["c_im"][l].rearrange("p (g c) -> p g c", c=16)))
    dvec = ph.sb([128, 32], F32)
    ph.dma("sp", dvec, dT(D["dvec"][l]))
    maskf = ph.sb([128, 2, 256], F32); maskb = ph.sb([128, 2, 256], F32)
    ph.dma("sp", maskf, dT(D["maskf"].rearrange("p (s c) -> p s c", s=2)))
    ph.dma("sp", maskb, dT(D["maskb"].rearrange("p (s c) -> p s c", s=2)))

    lr = ph.sb([128, 32], F32); dtt = ph.sb([128, 32], F32)
    ph.ts("dve", lr, lre, -1e-4, ALU.min)
    ph.act(dtt, ldt, AF.Exp)
    lrdt = ph.sb([128, 32], F32); lidt = ph.sb([128, 32], F32)
    ph.tt("dve", lrdt, lr, dtt, ALU.mult)
    ph.tt("dve", lidt, lim, dtt, ALU.mult)
    NE = 34
    E = ph.sb([128, 32, NE], F32); ANG = ph.sb([128, 32, NE], F32)
    mexpb = mexp.bc(1, [128, 32, NE])
    ph.tt("dve", E, lrdt.bc(2, [128, 32, NE]), mexpb, ALU.mult)
    ph.tt("dve", ANG, lidt.bc(2, [128, 32, NE]), mexpb, ALU.mult)
    mag = E
    ph.act(mag, E, AF.Exp)
    ki = ph.sb([128, 32, NE], I32); kf = ph.sb([128, 32, NE], F32); r = ph.sb([128, 32, NE], F32)
    msk = ph.sb([128, 32, NE], F32)
    ph.ts("dve", kf, ANG, 1.0 / TWO_PI, ALU.mult)
    ph.copy("dve", ki, kf)
    ph.copy("dve", kf, ki)
    ph.stt(r, kf, -TWO_PI, ANG, ALU.mult, ALU.add)

    def fold(x):
        ph.ts("dve", msk, x, math.pi, ALU.is_gt, -TWO_PI, ALU.mult)
        ph.tt("dve", x, x, msk, ALU.add)
        ph.ts("dve", msk, x, -math.pi, ALU.is_lt, TWO_PI, ALU.mult)
        ph.tt("dve", x, x, msk, ALU.add)

    fold(r)
    sn = ph.sb([128, 32, NE], F32); cs = ph.sb([128, 32, NE], F32)
    ph.act(sn, r, AF.Sin)
    ph.ts("dve", r, r, math.pi / 2, ALU.add)
    fold(r)
    ph.act(cs, r, AF.Sin)
    Are = cs; Aim = sn
    ph.tt("dve", Are, mag, cs, ALU.mult)
    ph.tt("dve", Aim, mag, sn, ALU.mult)
    a1r = Are[:, :, 33]; a1i = Aim[:, :, 33]
    nre = ph.sb([128, 32], F32); den = ph.sb([128, 32], F32); t1 = ph.sb([128, 32], F32); t2 = ph.sb([128, 32], F32)
    kre = ph.sb([128, 32], F32); kim = ph.sb([128, 32], F32)
    ph.ts("dve", nre, a1r, -1.0, ALU.add)
    ph.tt("dve", den, lr, lr, ALU.mult)
    ph.tt("dve", t1, lim, lim, ALU.mult)
    ph.tt("dve", den, den, t1, ALU.add)
    ph.recip(den, den)
    ph.tt("dve", t1, nre, lr, ALU.mult)
    ph.tt("dve", t2, a1i, lim, ALU.mult)
    ph.tt("dve", t1, t1, t2, ALU.add)
    ph.tt("dve", kre, t1, den, ALU.mult)
    ph.tt("dve", t1, a1i, lr, ALU.mult)
    ph.tt("dve", t2, nre, lim, ALU.mult)
    ph.tt("dve", t1, t1, t2, ALU.subtract)
    ph.tt("dve", kim, t1, den, ALU.mult)
    bbr = ph.sb([128, 32, 16], F32); bbi = ph.sb([128, 32, 16], F32)
    u1 = ph.sb([128, 32, 16], F32); u2 = ph.sb([128, 32, 16], F32)
    kreb = kre.bc(2, [128, 32, 16]); kimb = kim.bc(2, [128, 32, 16])
    ph.tt("dve", u1, bre, kreb, ALU.mult)
    ph.tt("dve", u2, bim, kimb, ALU.mult)
    ph.tt("dve", bbr, u1, u2, ALU.subtract)
    ph.tt("dve", u1, bim, kreb, ALU.mult)
    ph.tt("dve", u2, bre, kimb, ALU.mult)
    ph.tt("dve", bbi, u1, u2, ALU.add)
    dec = ph.sb([128, 2, 32, 2], F32)
    a16r = Are[:, :, 32]; a16i = Aim[:, :, 32]
    ph.copy("dve", dec[:, 0, :, 0], a16r)
    ph.copy("dve", dec[:, 0, :, 1], a16r)
    ph.ts("dve", dec[:, 1, :, 0], a16i, -1.0, ALU.mult)
    ph.copy("dve", dec[:, 1, :, 1], a16i)
    ph.dma("sp", dT(D["dec_s"][l].rearrange("p (a g r) -> p a g r", a=2, r=2)), dec)

    GB = 4
    Fre = ph.sb([128, GB, 16, 16], BF16); Fim = ph.sb([128, GB, 16, 16], BF16)
    Gre = ph.sb([128, GB, 16, 16], BF16); nGim = ph.sb([128, GB, 16, 16], BF16)
    v1 = ph.sb([128, GB, 16, 16], F32); v2 = ph.sb([128, GB, 16, 16], F32)
    w1 = ph.sb([128, GB, 16, 16], F32); w2 = ph.sb([128, GB, 16, 16], F32)
    mt32 = [ph.sb([128, 256], F32, name=f"mt32_{i}") for i in range(2)]
    mt32b = [ph.sb([128, 256], F32, name=f"mt32b_{i}") for i in range(2)]
    pack = [ph.sb([128, 1024], BF16, name=f"pack{i}") for i in range(2)]
    shp = [128, GB, 16, 16]
    for gb in range(32 // GB):
        gs = slice(gb * GB, (gb + 1) * GB)
        aFr = T(Are.ap[:, gs, 0:16].unsqueeze(3).to_broadcast(shp), Are.buf)
        aFi = T(Aim.ap[:, gs, 0:16].unsqueeze(3).to_broadcast(shp), Aim.buf)
        bR = T(bbr.ap[:, gs, :].unsqueeze(2).to_broadcast(shp), bbr.buf)
        bI = T(bbi.ap[:, gs, :].unsqueeze(2).to_broadcast(shp), bbi.buf)
        ph.tt("dve", v1, aFr, bR, ALU.mult)
        ph.tt("dve", v2, aFi, bI, ALU.mult)
        ph.tt("dve", Fre, v1, v2, ALU.subtract)
        ph.tt("dve", v1, aFr, bI, ALU.mult)
        ph.tt("dve", v2, aFi, bR, ALU.mult)
        ph.tt("dve", Fim, v1, v2, ALU.add)
        aGr = T(Are.ap[:, gs, 16:32].unsqueeze(3).to_broadcast(shp), Are.buf)
        aGi = T(Aim.ap[:, gs, 16:32].unsqueeze(3).to_broadcast(shp), Aim.buf)
        cR = T(cre.ap[:, gs, :].unsqueeze(2).to_broadcast(shp), cre.buf)
        cI = T(cim.ap[:, gs, :].unsqueeze(2).to_broadcast(shp), cim.buf)
        ph.tt("dve", w1, aGr, cR, ALU.mult)
        ph.tt("dve", w2, aGi, cI, ALU.mult)
        ph.tt("dve", Gre, w1, w2, ALU.subtract)
        ph.tt("dve", w1, aGr, cI, ALU.mult)
        ph.tt("dve", w2, aGi, cR, ALU.mult)
        ph.tt("dve", w1, w1, w2, ALU.add)
        ph.ts("dve", nGim, w1, -1.0, ALU.mult)
        for gi in range(GB):
            gg = gb * GB + gi
            pk = pack[gg % 2]
            dst = D["ssm_s"][l][gg]
            for sh in range(2):
                pp = []
                for d in range(2):
                    p = pm[d]
                    ps_ = slice(64 * d, 64 * d + 64)
                    lre_ = Fre[ps_, gi, 8 * sh:8 * sh + 8, :].re("p s c -> p (s c)")
                    lim_ = Fim[ps_, gi, 8 * sh:8 * sh + 8, :].re("p s c -> p (s c)")
                    rre_ = Gre[ps_, gi, :, :].re("p t c -> p (t c)")
                    rim_ = nGim[ps_, gi, :, :].re("p t c -> p (t c)")
                    ph.matmul(p[:, 0:256], lre_, rre_, start=True, stop=False)
                    ph.matmul(p[:, 0:256], lim_, rim_, start=False, stop=True)
                    pp.append(p)
                a = mt32[sh]
                b = mt32b[sh]
                ph.tt("dve", a, pp[0][:, 0:256], maskf[:, sh, :], ALU.mult)
                ph.stt(a[:, sh * 128:(sh + 1) * 128], identf, dvec[:, gg:gg + 1], a[:, sh * 128:(sh + 1) * 128], ALU.mult, ALU.add)
                ph.tt("dve", b, pp[1][:, 0:256], maskb[:, sh, :], ALU.mult)
                ph.tt("pool", pk[:, sh * 256:(sh + 1) * 256], a, b, ALU.add)
            pt = pm[2].bitcast(BF16)
            for sh in range(2):
                for ri, Fx in enumerate((Fre, Fim)):
                    ph.transpose(pt[:, (sh * 2 + ri) * 128:(sh * 2 + ri + 1) * 128],
                                 Fx[:, gi, 8 * sh:8 * sh + 8, :].re("p s c -> p (s c)"), identb)
            ph.copy("act", pk[:, 512:1024], pt[:, 0:512])
            ph.dma("sp", dT(dst[:, 0:1024]), pk)
            ph.dma("sp", dT(dst[:, 1024:1280]), Gre[:, gi, :, :].re("p t c -> p (t c)"))
            ph.dma("sp", dT(dst[:, 1280:1536]), nGim[:, gi, :, :].re("p t c -> p (t c)"))
    ph.finish()


def phase_ab(nc, l, G, D, dbg):
    import os
    ph = Phase(nc, f"A{l}")
    g = wrapG(ph, G)
    identb, identf, mhalf, e0 = g["identb"], g["identf"], g["mhalf"], g["e0"]
    xsrc = D["x"] if l == 0 else D["xcur"]
    win = ph.sb([128, 8, 1280], BF16)
    ph.dma("sp", win, dT(D["win_s"][l].rearrange("(kt p) n -> p kt n", p=128)))
    bz = ph.sb([128, 1280], BF16)
    ph.memset("pool", bz, 0.0)
    ph.dma("sp", bz[0:1, :], dT(D["bz_s"][l]))
    gqk = ph.sb([128, 640], F32)
    ph.dma("sp", gqk, dT(D["gain_qk"][l].broadcast_to([128, 640])))
    cosT = ph.sb([128, NT, 32], F32); sinT = ph.sb([128, NT, 32], F32)
    ph.dma("sp", cosT, dT(D["ropec"].rearrange("p (i j) -> p i j", j=32)))
    ph.dma("sp", sinT, dT(D["ropes"].rearrange("p (i j) -> p i j", j=32)))
    qT = ph.sb([128, 4, SEQ], BF16)
    kTp = [ph.sb([128, SEQ], BF16, name=f"kTp{i}") for i in range(2)]
    ph.memset("pool", kTp[0][64:128, :], 0.0)
    ph.memset("pool", kTp[1][0:64, :], 0.0)
    Vx = ph.sb([128, NT, 2, 65], BF16)
    ph.memset("pool", Vx[:, :, :, 64:65], 1.0)
    S = [ph.ps([128, 1024], F32, name=f"S{i}") for i in range(3)]
    PA = ph.ps([128, 512], F32, name="PA")
    PB = ph.ps([128, 512], F32, name="PB")
    Z = [S[0][:, 0:512], S[0][:, 512:1024], S[1][:, 0:512]]

    def two(shape, dt, nm):
        return [ph.sb(shape, dt, name=f"{nm}{i}") for i in range(2)]
    xb = [ph.sb([128, DM], F32, name=f"xb{i}") for i in range(3)]
    junk_ = two([128, DM], BF16, "junk"); xn_ = two([128, DM], BF16, "xn")
    xnT = two([128, 8, 128], BF16, "xnT")
    ss_ = two([128, 1], F32, "ss"); rs_ = two([128, 1], F32, "rs")
    zqk_ = [ph.sb([128, 640], F32, name=f"zqk{i}") for i in range(3)]
    sq_ = two([128, 640], F32, "sq"); qn_ = two([128, 640], F32, "qn")
    ssh_ = two([128, 10], F32, "ssh"); rh_ = two([128, 10], F32, "rh")
    r1_ = two([128, 10, 32], F32, "r1"); r2_ = two([128, 10, 32], F32, "r2")
    r3_ = two([128, 10, 32], F32, "r3"); r4_ = two([128, 10, 32], F32, "r4")
    qr_ = two([128, 640], BF16, "qr")
    ub = two([128, 512], BF16, "ub")
    if dbg:
        zfb = ph.sb([128, 1280], F32, name="zfb")
    NTA = int(os.environ.get("NTA", NT)); ATT = int(os.environ.get("ATT", NT))

    def load_x(i):
        ph.dma("sp", xb[i % 3], dT(xsrc[i * 128:(i + 1) * 128, :]))

    def stage1(i):
        b_ = i % 2
        xt, junk, xn, ss, rs, zqk = xb[i % 3], junk_[b_], xn_[b_], ss_[b_], rs_[b_], zqk_[i % 3]
        if i + 2 < NTA:
            load_x(i + 2)
        ph.act(junk, xt, AF.Square, accum_out=ss)
        ph.rstd(rs, ss, float(DM), mhalf[:, 0:1], mode="sqrt")
        ph.act(xn, xt, AF.Copy, scale=rs)
        pT = PA.bitcast(BF16)
        for kt in range(8):
            ph.transpose(pT[:, kt * 128:(kt + 1) * 128], xn[:, kt * 128:(kt + 1) * 128], identb)
        xT = xnT[b_]
        ph.copy("dve", xT.re("p k t -> p (k t)"), pT)
        for ci, (c0, cw) in enumerate(((0, 512), (512, 512), (1024, 256))):
            for kt in range(8):
                ph.matmul(Z[ci][:, :cw], xT[:, kt, :], win[:, kt, c0:c0 + cw], start=(kt == 0), stop=False)
            ph.matmul(Z[ci][:, :cw], e0, bz[:, c0:c0 + cw], start=False, stop=True)
        ph.copy("act", zqk[:, 0:512], Z[0])
        ph.copy("act", zqk[:, 512:640], Z[1][:, 0:128])
        ph.copy("dve", Vx[:, i, :, 0:64], Z[1][:, 128:256].re("p (g d) -> p g d", g=2))
        u_ = ub[b_]
        ph.copy("act", u_[:, 0:256], Z[1][:, 256:512])
        ph.copy("act", u_[:, 256:512], Z[2][:, 0:256])
        ph.dma("sp", dT(D["u_scr"][i * 128:(i + 1) * 128, :]), u_)
        if dbg:
            zf = zfb
            ph.copy("dve", zf[:, 0:640], zqk)
            ph.copy("dve", zf[:, 640:1024], Z[1][:, 128:512])
            ph.copy("dve", zf[:, 1024:1280], Z[2][:, 0:256])
            ph.dma("sp", dT(D["dbg_z"][i * 128:(i + 1) * 128, :]), zf)

    def stage2(i):
        b_ = i % 2
        zqk, sq, qn, ssh, rh = zqk_[i % 3], sq_[b_], qn_[b_], ssh_[b_], rh_[b_]
        r1, r2, r3, r4, qr = r1_[b_], r2_[b_], r3_[b_], r4_[b_], qr_[b_]
        ph.tt("dve", sq, zqk, zqk, ALU.mult)
        ph.reduce(ssh, sq.re("p (h d) -> p h d", d=64))
        ph.rstd(rh, ssh, 64.0, mhalf[:, 0:10], mode="sqrt")
        ph.tt("dve", qn.re("p (h d) -> p h d", d=64), zqk.re("p (h d) -> p h d", d=64), rh.bc(2, [128, 10, 64]), ALU.mult)
        ph.tt("dve", qn, qn, gqk, ALU.mult)
        q4 = qn.re("p (h j two) -> p h j two", h=10, two=2)
        x1 = q4[:, :, :, 0]; x2 = q4[:, :, :, 1]
        csb = cosT[:, i, :].bc(1, [128, 10, 32]); snb = sinT[:, i, :].bc(1, [128, 10, 32])
        o4 = qr.re("p (h j two) -> p h j two", h=10, two=2)
        ph.tt("dve", r1, x1, csb, ALU.mult)
        ph.tt("dve", r2, x2, snb, ALU.mult)
        ph.tt("dve", o4[:, :, :, 0], r1, r2, ALU.subtract)
        ph.tt("dve", r3, x1, snb, ALU.mult)
        ph.tt("dve", r4, x2, csb, ALU.mult)
        ph.tt("dve", o4[:, :, :, 1], r3, r4, ALU.add)
        pq = PB.bitcast(BF16)
        for j in range(4):
            ph.transpose(pq[:, j * 128:(j + 1) * 128], qr[:, j * 128:(j + 1) * 128], identb)
        ph.transpose(pq[:, 512:640], qr[:, 512:640], identb)
        ph.copy("dve", qT[:, :, i * 128:(i + 1) * 128], pq[:, 0:512].re("p (j t) -> p j t", j=4))
        ph.copy("act", kTp[0][0:64, i * 128:(i + 1) * 128], pq[0:64, 512:640])
        ph.copy("act", kTp[1][64:128, i * 128:(i + 1) * 128], pq[64:128, 512:640])

    for i in range(min(2, NTA)):
        load_x(i)
    for i in range(NTA):
        stage1(i)
        if i >= 2:
            stage2(i - 2)
    for i in range(max(0, NTA - 2), NTA):
        stage2(i)

    PT = [ph.sb([128, 2, 4, 128], BF16, name=f"PT{i}") for i in range(3)]
    ot = two([65, 512], F32, "ot")
    oat = two([128, 512], F32, "oat")
    oan = two([128, 512], BF16, "oan")
    mT = two([128, 4, 128], BF16, "mT")
    rec = two([128, 8], F32, "rec")
    ss2 = two([128, 1], F32, "ss2"); rs2 = two([128, 1], F32, "rs2")
    junk2 = two([128, 512], BF16, "junk2")
    steps = [(i, gq, kg) for i in range(ATT) for gq in range(2) for kg in range(16)]
    deferred = []

    def qk(n):
        i, gq, kg = steps[n]
        pr = slice(64 * gq, 64 * gq + 64)
        Sx = S[n % 3]
        for kk in range(2):
            kt = 2 * kg + kk
            ph.matmul(Sx[:, kk * 512:(kk + 1) * 512], kTp[gq][:, kt * 128:(kt + 1) * 128],
                      qT[:, :, i * 128:(i + 1) * 128], start=True, stop=True)

    def evac(i, gq):
        b = i % 2
        c = (2 * i + gq) % 2
        pe_ = PB[:, 0:260].re("p (j d) -> p j d", j=4)
        for j in range(4):
            ph.transpose(pe_[:, j, :], ot[c][0:65, j * 128:(j + 1) * 128], identf[0:65, 0:65])
        ph.recip(rec[b][:, 4 * gq:4 * gq + 4], pe_[:, :, 64])
        ph.tt("dve", oat[b][:, 256 * gq:256 * gq + 256].re("p (j d) -> p j d", j=4), pe_[:, :, 0:64],
              rec[b][:, 4 * gq:4 * gq + 4].bc(2, [128, 4, 64]), ALU.mult)
        if gq == 1:
            ph.act(junk2[b], oat[b], AF.Square, accum_out=ss2[b])
            ph.rstd(rs2[b], ss2[b], 512.0, mhalf[:, 0:1], mode="pow")
            ph.ts("dve", oan[b], oat[b], rs2[b], ALU.mult)

    def finalize(i):
        b = i % 2
        pmx = PB.bitcast(BF16)[:, 0:512]
        for k in range(4):
            ph.transpose(pmx[:, k * 128:(k + 1) * 128], oan[b][:, k * 128:(k + 1) * 128], identb)
        ph.copy("dve", mT[b].re("p k t -> p (k t)"), pmx)
        ph.dma("sp", dT(D["mixa_scr"][:, :, i * 128:(i + 1) * 128]), mT[b])

    def pv(n):
        i, gq, kg = steps[n]
        PTx = PT[n % 3]
        c = (2 * i + gq) % 2
        acc = PA[0:65, :]
        for kk in range(2):
            kt = 2 * kg + kk
            ph.matmul(acc, Vx[:, kt, gq, :], PTx[:, kk, :, :], start=(kt == 0), stop=(kt == NT - 1))
        if kg == 15:
            ph.copy("dve", ot[c], acc)
            deferred.append((n + 2, "e", i, gq))
            if gq == 1:
                deferred.append((n + 4, "f", i, gq))

    def run_deferred(n):
        k = 0
        while k < len(deferred):
            if deferred[k][0] <= n:
                _, kind, i, gq = deferred.pop(k)
                if kind == "e":
                    evac(i, gq)
                else:
                    finalize(i)
            else:
                k += 1

    for n in range(min(2, len(steps))):
        qk(n)
    for n in range(len(steps)):
        if n + 2 < len(steps):
            qk(n + 2)
        ph.act(PT[n % 3].re("p a b c -> p (a b c)"), S[n % 3], AF.Exp, scale=0.125)
        pv(n)
        run_deferred(n)
    run_deferred(10 ** 9)
    ph.finish()


def phase_c(nc, l, G, D):
    ph = Phase(nc, f"C{l}")
    g = wrapG(ph, G)
    identb = g["identb"]
    DY = ph.sb([128, 2, 8192], BF16)
    us = D["u_scr"].rearrange("(kc k s) c -> kc k (s c)", kc=2, k=128, s=16)
    Uall = ph.sb([128, 32, 2, 256], BF16)
    XH = ph.sb([128, 32, 257, 2], F32)
    Hbf = ph.sb([128, 32, 257, 2], BF16)
    raw = Hbf.re("p g k r -> p (g k r)")
    for kc in range(2):
        rk = raw[:, kc * 8192:(kc + 1) * 8192]
        ph.dma("sp", rk, dT(us[kc]))
        ph.copy("dve", DY[:, kc, :].re("p (g s c) -> p g s c", g=32, s=16),
                rk.re("p (s g c) -> p g s c", s=16, g=32))
    dec = ph.sb([128, 2, 32, 2], F32)
    ph.dma("sp", dec, dT(D["dec_s"][l].rearrange("p (a g r) -> p a g r", a=2, r=2)))
    ph.memset("pool", XH[:, :, 0, :], 0.0)
    mt = [ph.sb([128, 1536], BF16, name=f"mt{i}") for i in range(3)]
    PS = [ph.ps([128, 512], F32, name=f"PS{i}") for i in range(8)]
    DYv = [DY[:, kc, :].re("p (s c) -> p s c", c=512) for kc in range(2)]

    def load_m(gg):
        ph.dma("sp", mt[gg % 3], dT(D["ssm_s"][l][gg]))

    load_m(0)
    load_m(1)
    for gg in range(32):
        if gg + 2 < 32:
            load_m(gg + 2)
        m = mt[gg % 3]
        msum = m[:, 512:1024].re("p (sh ri q) -> p sh ri q", sh=2, ri=2)
        pu = PS[gg % 2].bitcast(BF16)
        for sh in range(2):
            for kc in range(2):
                ph.transpose(pu[:, sh * 256 + kc * 128: sh * 256 + (kc + 1) * 128],
                             DY[:, kc, gg * 256 + sh * 128: gg * 256 + (sh + 1) * 128], identb)
        ph.copy("act" if gg % 2 == 0 else "pool", Uall[:, gg, :, :].re("p s k -> p (s k)"), pu[:, 0:512]) if False else \
            ph.copy("act", Uall[:, gg, :, :].re("p s k -> p (s k)"), pu[:, 0:512])
        px = PS[2 + gg % 2]
        pxv = px.re("p (ri k) -> p ri k", ri=2)
        for ri in range(2):
            for sh in range(2):
                ph.matmul(pxv[:, ri, :], msum[:, sh, ri, :], Uall[:, gg, sh, :], start=(sh == 0), stop=(sh == 1))
        ph.copy("dve", XH[0:64, gg, 1:257, :], pxv[0:64].re("p ri k -> p k ri"))
        pxr = T(pxv.ap[64:128, :, ::-1], pxv.buf)
        ph.copy("dve", XH[64:128, gg, 1:257, :], pxr.re("p ri k -> p k ri"))
    R = ph.sb([128, 32, 2], F32); Pm = ph.sb([128, 32, 2], F32); Qm = ph.sb([128, 32, 2], F32)
    Rsw = T(R.ap[:, :, ::-1], R.buf)
    AA = dec[:, 0]; AB = dec[:, 1]
    for k in range(256):
        ph.tt("dve", R, XH[:, :, k, :], XH[:, :, k + 1, :], ALU.add)
        ph.tt("dve", Pm, AA, R, ALU.mult)
        ph.tt("dve", Qm, AB, Rsw, ALU.mult)
        ph.tt("dve", XH[:, :, k + 1, :], Pm, Qm, ALU.add)
    for q in range(4):
        gsl = slice(8 * q, 8 * q + 8)
        ph.copy("act", Hbf[0:64, gsl, 0:256, :], XH[0:64, gsl, 0:256, :])
        xr = T(XH.ap[64:128, gsl, 255::-1, :], XH.buf)
        ph.copy("dve", Hbf[64:128, gsl, 0:256, :], xr)
    load_m(0)
    load_m(1)
    x2 = [ph.sb([128, 512], F32, name=f"x2_{i}") for i in range(2)]
    ygb = [ph.sb([128, 2, 256], BF16, name=f"ygb{i}") for i in range(2)]
    for gg in range(32):
        if gg + 2 < 32:
            load_m(gg + 2)
        m = mt[gg % 3]
        mintra = m[:, 0:512].re("p (sh q) -> p sh q", sh=2)
        mout = m[:, 1024:1536].re("p (ri q) -> p ri q", ri=2)
        py = PS[4 + gg % 2]
        pyv = py.re("p (th k) -> p th k", th=2)
        for th in range(2):
            for sh in range(2):
                ph.matmul(pyv[:, th, :], mintra[:, sh, th * 128:(th + 1) * 128], Uall[:, gg, sh, :], start=(sh == 0), stop=False)
            for ri in range(2):
                ph.matmul(pyv[:, th, :], mout[:, ri, th * 128:(th + 1) * 128], Hbf[:, gg, 0:256, ri], start=False, stop=(ri == 1))
        a = x2[gg % 2]
        yb = ygb[gg % 2]
        ph.act(a, py, AF.Square)
        ph.ts("dve", a, a, 0.044715, ALU.mult, 1.0, ALU.add)
        ph.tt("dve", a, a, py, ALU.mult)
        ph.act(a, a, AF.Tanh, scale=0.7978845608028654)
        ph.ts("dve", a, a, 0.5, ALU.mult, 0.5, ALU.add)
        ph.tt("dve", yb.re("p th k -> p (th k)"), a, py, ALU.mult)
        ptb = PS[6 + gg % 2].bitcast(BF16)
        for th in range(2):
            for kc in range(2):
                ph.transpose(ptb[:, (th * 2 + kc) * 128:(th * 2 + kc + 1) * 128], yb[:, th, kc * 128:(kc + 1) * 128], identb)
        ptv = ptb[:, 0:512].re("p (th kc t8 c) -> p th kc t8 c", th=2, kc=2, t8=8)
        for kc in range(2):
            outv = DYv[kc][:, :, 16 * gg:16 * gg + 16].re("p (th t8) c -> p th t8 c", th=2)
            ph.copy("act" if kc == 0 else "dve", outv, ptv[:, :, kc, :, :])
    ys = D["yg_scr"].rearrange("(kc k s) c -> kc k (s c)", kc=2, k=128, s=16)
    for kc in range(2):
        ph.dma("sp", dT(ys[kc]), DY[:, kc, :])
    ph.finish()


def phase_d1(nc, l, G, D):
    ph = Phase(nc, f"D{l}")
    g = wrapG(ph, G)
    identb, mhalf, e0 = g["identb"], g["mhalf"], g["e0"]
    xsrc = D["x"] if l == 0 else D["xcur"]
    wglu = ph.sb([128, 4, 512], BF16)
    ph.dma("sp", wglu, dT(D["wglu_s"][l].rearrange("(kt p) n -> p kt n", p=128)))
    wout = ph.sb([128, 8, DM], BF16)
    ph.dma("sp", wout, dT(D["wout_s"][l].rearrange("(kt p) n -> p kt n", p=128)))
    bgf = ph.sb([1, 512], F32); bgl = ph.sb([128, 512], BF16)
    ph.memset("pool", bgl, 0.0)
    ph.dma("sp", bgf, dT(D["b_glu_row"][l]))
    ph.copy("dve", bgl[0:1, :], bgf)
    PS = [ph.ps([128, 512], F32, name=f"PS{i}") for i in range(8)]

    def two(shape, dt, nm, n=2):
        return [ph.sb(shape, dt, name=f"{nm}{i}") for i in range(n)]
    ygt = two([128, 512], BF16, "ygt", 3); mTa = two([128, 4, 128], BF16, "mTa", 3); xb = two([128, DM], F32, "xb", 3)
    ygT = two([128, 4, 128], BF16, "ygT"); mTs = two([128, 4, 128], BF16, "mTs"); xo = two([128, DM], F32, "xo")
    sg_ = two([128, 512], F32, "sg"); o_ = two([128, 512], F32, "o"); on_ = two([128, 512], BF16, "on")
    junk_ = two([128, 512], BF16, "junk"); ss_ = two([128, 1], F32, "ss"); rs_ = two([128, 1], F32, "rs")

    def load(i):
        rows = slice(i * 128, (i + 1) * 128)
        ph.dma("sp", ygt[i % 3], dT(D["yg_scr"][rows, :]))
        ph.dma("sp", mTa[i % 3], dT(D["mixa_scr"][:, :, rows]))
        ph.dma("sp", xb[i % 3], dT(xsrc[rows, :]))

    def stage1(i):
        b = i % 2
        yg = ygt[i % 3]
        sg, o, on, junk, ss, rs = sg_[b], o_[b], on_[b], junk_[b], ss_[b], rs_[b]
        pt = PS[0 + b].bitcast(BF16)
        for k in range(4):
            ph.transpose(pt[:, k * 128:(k + 1) * 128], yg[:, k * 128:(k + 1) * 128], identb)
        ph.copy("dve", ygT[b].re("p k t -> p (k t)"), pt[:, 0:512])
        pg = PS[2 + b]
        for k in range(4):
            ph.matmul(pg, ygT[b][:, k, :], wglu[:, k, :], start=(k == 0), stop=False)
        ph.matmul(pg, e0, bgl, start=False, stop=True)
        ph.act(sg, pg, AF.Tanh, scale=0.5)
        ph.ts("dve", sg, sg, 0.5, ALU.mult, 0.5, ALU.add)
        ph.tt("dve", o, sg, yg, ALU.mult)
        ph.act(junk, o, AF.Square, accum_out=ss)
        ph.rstd(rs, ss, 512.0, mhalf[:, 0:1])
        ph.act(on, o, AF.Copy, scale=rs)

    def stage2(i):
        b = i % 2
        rows = slice(i * 128, (i + 1) * 128)
        pt2 = PS[4 + b].bitcast(BF16)
        for k in range(4):
            ph.transpose(pt2[:, k * 128:(k + 1) * 128], on_[b][:, k * 128:(k + 1) * 128], identb)
        ph.copy("dve", mTs[b].re("p k t -> p (k t)"), pt2[:, 0:512])
        for c2 in range(2):
            po = PS[6 + c2]
            for k in range(8):
                lhsT = mTa[i % 3][:, k, :] if k < 4 else mTs[b][:, k - 4, :]
                ph.matmul(po, lhsT, wout[:, k, c2 * 512:(c2 + 1) * 512], start=(k == 0), stop=(k == 7))
            ph.tt("dve", xo[b][:, c2 * 512:(c2 + 1) * 512], po, xb[i % 3][:, c2 * 512:(c2 + 1) * 512], ALU.add)
        ph.dma("sp", dT(D["xmid"][rows, :]), xo[b])

    load(0)
    load(1)
    for i in range(NT):
        stage1(i)
        if i >= 1:
            stage2(i - 1)
        if i + 2 < NT:
            load(i + 2)
    stage2(NT - 1)
    ph.finish()


def phase_d2(nc, l, G, D, last):
    ph = Phase(nc, f"F{l}")
    g = wrapG(ph, G)
    identb, mhalf = g["identb"], g["mhalf"]
    wfi = ph.sb([128, 8, 2 * FH], BF16)
    wfiv = D["wfi_s"][l].rearrange("(kt p) n -> p kt n", p=128)
    for kt in range(8):
        ph.dma("sp", wfi[:, kt, :], dT(wfiv[:, kt, :]))
    wfo = ph.sb([128, 22, DM], BF16)
    wfov = D["wfo_s"][l].rearrange("(ft p) n -> p ft n", p=128)
    for f0 in range(0, 22, 6):
        nf = min(6, 22 - f0)
        ph.dma("sp", wfo[:, f0:f0 + nf, :], dT(wfov[:, f0:f0 + nf, :]))
    bfi = ph.sb([128, 44], F32); hbg = ph.sb([128, 22], F32)
    ph.dma("sp", bfi, dT(D["bfi_s"][l]))
    ph.ts("dve", hbg, bfi[:, 0:22], 0.5, ALU.mult)
    if last:
        fnb = ph.sb([128, DM], F32)
        ph.dma("sp", fnb, dT(D["fnorm_row"].broadcast_to([128, DM])))
    PS = [ph.ps([128, 512], F32, name=f"PS{i}") for i in range(8)]
    TB = 2
    xb = [ph.sb([128, DM], F32, name=f"xb{i}") for i in range(2 * TB)]
    xo = [ph.sb([128, DM], F32, name=f"xo{i}") for i in range(2)]
    xn = ph.sb([128, DM], BF16); junk = ph.sb([128, DM], BF16)
    xnT = [ph.sb([128, 8, TB * 128], BF16, name=f"xnT{i}") for i in range(2)]
    actT = [ph.sb([128, 22, TB * 128], BF16, name=f"actT{i}") for i in range(2)]
    ss = ph.sb([128, 1], F32); rs = ph.sb([128, 1], F32)
    thb = [ph.sb([128, TB * 128], F32, name=f"thb{i}") for i in range(2)]
    gbb = [ph.sb([128, TB * 128], F32, name=f"gbb{i}") for i in range(2)]
    sb_ = [ph.sb([128, TB * 128], F32, name=f"sb{i}") for i in range(2)]
    dst = D["out"] if last else D["xcur"]
    W = TB * 128
    nmm = 0
    def load_blk(blk):
        for tt in range(TB):
            i = blk * TB + tt
            ph.dma("sp", xb[(blk % 2) * TB + tt], dT(D["xmid"][i * 128:(i + 1) * 128, :]))

    def prep(blk):
        bb = blk % 2
        for tt in range(TB):
            xt = xb[bb * TB + tt]
            ph.act(junk, xt, AF.Square, accum_out=ss)
            ph.rstd(rs, ss, float(DM), mhalf[:, 0:1])
            ph.act(xn, xt, AF.Copy, scale=rs)
            pT = PS[0].bitcast(BF16)
            for kt in range(8):
                ph.transpose(pT[:, kt * 128:(kt + 1) * 128], xn[:, kt * 128:(kt + 1) * 128], identb)
            ph.copy("dve", xnT[bb][:, :, tt * 128:(tt + 1) * 128], pT.re("p (k t) -> p k t", k=8))

    load_blk(0)
    prep(0)
    for blk in range(NT // TB):
        bb = blk % 2
        if blk + 1 < NT // TB:
            load_blk(blk + 1)
        for f in range(22):
            pgu = PS[1 + nmm % 2]
            nmm += 1
            pg = pgu[:, 0:W]; pu = pgu[:, W:2 * W]
            for kt in range(8):
                ph.matmul(pg, wfi[:, kt, f * 128:(f + 1) * 128], xnT[bb][:, kt, :], start=(kt == 0), stop=(kt == 7))
            for kt in range(8):
                ph.matmul(pu, wfi[:, kt, FH + f * 128:FH + (f + 1) * 128], xnT[bb][:, kt, :], start=(kt == 0), stop=(kt == 7))
            t_ = thb[f % 2]; g_ = gbb[f % 2]; s_ = sb_[f % 2]
            ph.act(t_, pg, AF.Tanh, scale=0.5, bias=hbg[:, f:f + 1])
            ph.ts("pool", g_, pg, bfi[:, f:f + 1], ALU.add) if False else ph.act(g_, pg, AF.Identity, bias=bfi[:, f:f + 1])
            ph.stt(s_, t_, 1.0, g_, ALU.add, ALU.mult)
            ph.stt(actT[bb][:, f, :], pu, bfi[:, 22 + f:23 + f], s_, ALU.add, ALU.mult)
        if blk + 1 < NT // TB:
            prep(blk + 1)
        for tt in range(TB):
            i = blk * TB + tt
            xt = xb[bb * TB + tt]
            xo_ = xo[tt % 2]
            for c2 in range(2):
                py = PS[3 + 2 * (tt % 2) + c2]
                for f in range(22):
                    ph.matmul(py, actT[bb][:, f, tt * 128:(tt + 1) * 128], wfo[:, f, c2 * 512:(c2 + 1) * 512],
                              start=(f == 0), stop=(f == 21))
                ph.tt("dve", xo_[:, c2 * 512:(c2 + 1) * 512], py, xt[:, c2 * 512:(c2 + 1) * 512], ALU.add)
            if last:
                ph.act(junk, xo_, AF.Square, accum_out=ss)
                ph.rstd(rs, ss, float(DM), mhalf[:, 0:1])
                ph.stt(xo_, xo_, rs, fnb, ALU.mult, ALU.mult)
            ph.dma("sp", dT(dst[i * 128:(i + 1) * 128, :]), xo_)
    ph.finish()


QPERM = np.concatenate([np.arange(64) + 64 * (g * 4 + j) for j in range(4) for g in range(2)] + [np.arange(512, 1280)])


def _consts():
    ident = np.eye(128, dtype=np.float32)
    p = np.arange(128)
    s8 = p // 16
    col = np.arange(256) // 16
    maskf = np.zeros((128, 2, 256), np.float32)
    maskb = np.zeros((128, 2, 256), np.float32)
    for sh in range(2):
        s = s8 + 8 * sh
        maskf[:, sh, :] = (col[None, :] >= s[:, None])
        maskb[:, sh, :] = (col[None, :] <= s[:, None])
    t = np.arange(SEQ)
    inv = (1.0 / (10000.0 ** (np.arange(0, 32, 2, dtype=np.float32) / 32.0))).astype(np.float32)
    row = (t // 64).astype(np.float32); cc = (t % 64).astype(np.float32)
    ang = np.concatenate([row[:, None] * inv[None, :], cc[:, None] * inv[None, :]], axis=-1).astype(np.float32)
    cosv = np.cos(ang).astype(np.float32); sinv = np.sin(ang).astype(np.float32)
    ropec = cosv.reshape(NT, 128, 32).transpose(1, 0, 2).reshape(128, NT * 32)
    ropes = sinv.reshape(NT, 128, 32).transpose(1, 0, 2).reshape(128, NT * 32)
    mexp = np.zeros((128, 34), np.float32)
    m = np.arange(16, dtype=np.float32)
    mexp[:64, 0:16] = -m; mexp[64:, 0:16] = m
    mexp[:64, 16:32] = m; mexp[64:, 16:32] = -m
    mexp[:, 32] = 16.0; mexp[:, 33] = 1.0
    return dict(ident=ident, maskf=maskf.reshape(128, 512), maskb=maskb.reshape(128, 512),
                ropec=np.ascontiguousarray(ropec), ropes=np.ascontiguousarray(ropes), mexp=mexp)


def _layout(inp):
    L = 4
    f = lambda a: np.ascontiguousarray(a, dtype=np.float32)
    col = lambda v: f(v.reshape(L, -1, 128).transpose(0, 2, 1))
    sh = {}
    sh["w_ada"] = f(inp["w_ada"]); sh["b_ada_row"] = f(inp["b_ada"].reshape(L, 1, 6144))
    sh["norm1_col"] = col(inp["norm1"]); sh["norm2_col"] = col(inp["norm2"])
    sh["mixn_col"] = col(np.concatenate([inp["attn_out_norm"], inp["ssm_out_norm"]], axis=-1))
    for k in ("w_out", "w_ffn_in", "w_ffn_out", "w_glu"):
        sh[k] = f(inp[k])
    sh["w_in"] = f(inp["w_in"][:, :, QPERM])
    sh["b_glu_row"] = f(inp["b_glu"].reshape(L, 1, 512))
    sh["gain_qk"] = f(np.concatenate([np.tile(inp["q_norm"], (1, 8)), np.tile(inp["k_norm"], (1, 2))], axis=-1).reshape(L, 1, 640))
    sh["fnorm_row"] = f(inp["final_norm"].reshape(1, DM))
    dp = lambda a: f(a.transpose(0, 1, 3, 2).reshape(L, 128, 32))
    sh["lam_re"] = dp(inp["ssm_lam_re"]); sh["lam_im"] = dp(inp["ssm_lam_im"])
    sh["log_dt"] = f(np.repeat(inp["ssm_log_dt"][:, :, None, :], 64, axis=2).reshape(L, 128, 32))
    bl = lambda a: f(a.transpose(0, 1, 3, 2, 4).reshape(L, 128, 512))
    cl = lambda a: f(a.transpose(0, 1, 4, 2, 3).reshape(L, 128, 512))
    sh["b_re"] = bl(inp["ssm_b_re"]); sh["b_im"] = bl(inp["ssm_b_im"])
    sh["c_re"] = cl(inp["ssm_c_re"]); sh["c_im"] = cl(inp["ssm_c_im"])
    dv = inp["ssm_d"].reshape(L, 32, 16)
    sh["dvec"] = f(np.tile(dv.transpose(0, 2, 1)[:, None, :, :], (1, 8, 1, 1)).reshape(L, 128, 32))
    sh.update(_consts())
    return sh


_CACHE = {}


def kernel(**inputs):
    inp = {k: np.asarray(v) for k, v in inputs.items()}
    shared = _layout(inp)
    if "nc" not in _CACHE:
        _CACHE["nc"] = build(4, False)
    nc = _CACHE["nc"]
    in_maps = []
    for b in range(8):
        m = dict(shared)
        m["x"] = np.ascontiguousarray(inp["x"][b], dtype=np.float32)
        m["cT"] = np.ascontiguousarray(inp["c"][b].reshape(8, 128).T, dtype=np.float32)
        in_maps.append(m)
    res = run_bass_kernel_spmd(nc, in_maps, core_ids=list(range(8)))
    out = np.stack([np.asarray(r["out"], dtype=np.float32) for r in res.results], axis=0)
    return out
```
